# Optimizing a Trainium2 kernel written in Bass

```python
import jax, jax.numpy as jnp
from jax import lax
import numpy as np

D_MODEL = 1024
BATCH = 8
SEQ = 4096
DEPTH = 4

N_META = 16
FOX_HEAD_DIM = 128
FOX_WIDTH = D_MODEL
FOX_HEADS = FOX_WIDTH // FOX_HEAD_DIM
FOX_BLOCK = 128
MLSTM_WIDTH = 2 * D_MODEL
MLSTM_HEADS = 4
MLSTM_V_DIM = MLSTM_WIDTH // MLSTM_HEADS
MLSTM_QK_DIM = MLSTM_V_DIM // 2
MLSTM_CHUNK = 128
CONV_WIDTH = 4
EPS = 1e-6

SPLIT_SIZES = (
    FOX_WIDTH, FOX_WIDTH, FOX_WIDTH,
    FOX_HEADS,
    FOX_WIDTH,
    2 * MLSTM_HEADS * MLSTM_QK_DIM,
    MLSTM_WIDTH,
    MLSTM_HEADS, MLSTM_HEADS,
    MLSTM_WIDTH,
    MLSTM_WIDTH,
    D_MODEL, D_MODEL,
)
N_IN = sum(SPLIT_SIZES)
SPLIT_POINTS = tuple(int(p) for p in np.cumsum(SPLIT_SIZES)[:-1])

kernel_name = 'hybrid_fox_mlstm_block'


def rms_norm(x, gain):
    xf = x.astype(jnp.float32)
    y = xf * lax.rsqrt(jnp.mean(xf * xf, axis=-1, keepdims=True) + EPS)
    return (y * gain.astype(jnp.float32)).astype(x.dtype)


def causal_conv(x, w, b):
    y = lax.conv_general_dilated(x, w[:, None, :], window_strides=(1,),
                                 padding=[(CONV_WIDTH - 1, 0)],
                                 dimension_numbers=('NWC', 'WIO', 'NWC'),
                                 feature_group_count=x.shape[-1])
    return y + b


def forgetting_attention(q, k, v, log_f):
    L = q.shape[1]
    cum = jnp.cumsum(log_f, axis=1).transpose(0, 2, 1)
    scale = FOX_HEAD_DIM ** -0.5
    edges = [0] + list(range(N_META, L + 1, FOX_BLOCK))
    outs = []
    for s0, s1 in zip(edges[:-1], edges[1:]):
        scores = jnp.einsum('bqhd,bkhd->bhqk', q[:, s0:s1], k[:, :s1]).astype(jnp.float32) * scale
        decay = cum[:, :, s0:s1, None] - cum[:, :, None, :s1]
        causal = jnp.arange(s0, s1)[:, None] >= jnp.arange(s1)[None, :]
        p = jax.nn.softmax(jnp.where(causal, scores + decay, -jnp.inf), axis=-1)
        outs.append(jnp.einsum('bhqk,bkhd->bqhd', p.astype(v.dtype), v[:, :s1]))
    return jnp.concatenate(outs, axis=1)


def mlstm_chunk(state, chunk):
    c_mat, n_vec, m = state
    q, k, v, log_i, log_f = chunk
    T = q.shape[2]
    b = jnp.cumsum(log_f, axis=-1)
    causal = jnp.tril(jnp.ones((T, T), dtype=bool))
    log_d = jnp.where(causal, b[..., :, None] - b[..., None, :] + log_i[..., None, :], -jnp.inf)
    log_inter = b + m[..., None]
    m_t = jnp.maximum(log_inter, jnp.max(log_d, axis=-1))
    d = jnp.exp(log_d - m_t[..., None])
    inter = jnp.exp(log_inter - m_t)
    s = jnp.einsum('bhtd,bhsd->bhts', q, k) * d
    num = inter[..., None] * jnp.einsum('bhtd,bhde->bhte', q, c_mat) + jnp.einsum('bhts,bhse->bhte', s, v)
    den = inter * jnp.einsum('bhtd,bhd->bht', q, n_vec) + jnp.sum(s, axis=-1)
    h = num / jnp.maximum(jnp.abs(den), jnp.exp(-m_t))[..., None]
    b_last = b[..., -1]
    log_w = b_last[..., None] - b + log_i
    m_new = jnp.maximum(b_last + m, jnp.max(log_w, axis=-1))
    w = jnp.exp(log_w - m_new[..., None])
    decay = jnp.exp(b_last + m - m_new)
    kw = k * w[..., None]
    c_new = decay[..., None, None] * c_mat + jnp.einsum('bhsd,bhse->bhde', kw, v)
    n_new = decay[..., None] * n_vec + jnp.sum(kw, axis=2)
    return (c_new, n_new, m_new), h


def mlstm_scan(q, k, v, log_i, log_f):
    B, L, H, _ = q.shape
    n_real = L - N_META
    nc = n_real // MLSTM_CHUNK
    f32 = jnp.float32
    q, k, v = (t.astype(f32).transpose(0, 2, 1, 3) for t in (q, k, v))
    log_i, log_f = log_i.transpose(0, 2, 1), log_f.transpose(0, 2, 1)
    state = (jnp.zeros((B, H, MLSTM_QK_DIM, MLSTM_V_DIM), f32),
             jnp.zeros((B, H, MLSTM_QK_DIM), f32),
             jnp.zeros((B, H), f32))
    state, h_meta = mlstm_chunk(state, tuple(t[:, :, :N_META] for t in (q, k, v, log_i, log_f)))

    def to_chunks(t):
        t = t[:, :, N_META:]
        t = t.reshape((B, H, nc, MLSTM_CHUNK) + t.shape[3:])
        return jnp.moveaxis(t, 2, 0)

    _, h_real = lax.scan(mlstm_chunk, state, tuple(to_chunks(t) for t in (q, k, v, log_i, log_f)))
    h_real = jnp.moveaxis(h_real, 0, 2).reshape(B, H, n_real, MLSTM_V_DIM)
    return jnp.concatenate([h_meta, h_real], axis=2).transpose(0, 2, 1, 3)


def hybrid_layer(x, g_pre, g_post, w_in, b_fox_f, conv_w, conv_b, b_i, b_f, g_head, w_a, w_b, w_o):
    B, L, _ = x.shape
    h = rms_norm(x, g_pre)
    z = h @ w_in
    (fq, fk, fv, ff, fz, mqk, mv, mi, mf, mo, mz, ga, gb) = jnp.split(z, SPLIT_POINTS, axis=-1)

    log_fa = jax.nn.log_sigmoid((ff + b_fox_f).astype(jnp.float32))
    oa = forgetting_attention(fq.reshape(B, L, FOX_HEADS, FOX_HEAD_DIM),
                              fk.reshape(B, L, FOX_HEADS, FOX_HEAD_DIM),
                              fv.reshape(B, L, FOX_HEADS, FOX_HEAD_DIM), log_fa).reshape(B, L, FOX_WIDTH)
    ya = (oa * jax.nn.silu(fz)) @ w_a

    qk = jax.nn.silu(causal_conv(mqk, conv_w, conv_b))
    mq, mk = jnp.split(qk, 2, axis=-1)
    mq = mq.reshape(B, L, MLSTM_HEADS, MLSTM_QK_DIM) * (MLSTM_QK_DIM ** -0.5)
    mk = mk.reshape(B, L, MLSTM_HEADS, MLSTM_QK_DIM)
    log_ib = (mi + b_i).astype(jnp.float32)
    log_fb = jax.nn.log_sigmoid((mf + b_f).astype(jnp.float32))
    hb = mlstm_scan(mq, mk, mv.reshape(B, L, MLSTM_HEADS, MLSTM_V_DIM), log_ib, log_fb).astype(x.dtype)
    hb = jax.nn.sigmoid(mo).reshape(B, L, MLSTM_HEADS, MLSTM_V_DIM) * hb
    hb = rms_norm(hb, g_head.reshape(MLSTM_HEADS, MLSTM_V_DIM)).reshape(B, L, MLSTM_WIDTH)
    yb = (hb * jax.nn.silu(mz)) @ w_b

    merged = jax.nn.sigmoid(ga) * ya + jax.nn.sigmoid(gb) * yb
    return x + rms_norm(merged @ w_o, g_post)


def setup_inputs(seed: int = 0) -> dict:
    key = jax.random.key(seed)
    ks = jax.random.split(key, 14)

    def normal(k, shape, scale):
        return jax.random.normal(k, shape, jnp.float32) * scale

    qk_cols = 2 * MLSTM_HEADS * MLSTM_QK_DIM
    return {
        'x': normal(ks[0], (BATCH, SEQ, D_MODEL), 1.0),
        'meta_tokens': normal(ks[1], (N_META, D_MODEL), 1.0),
        'norm_pre': 1.0 + normal(ks[2], (DEPTH, D_MODEL), 0.02),
        'norm_post': 1.0 + normal(ks[3], (DEPTH, D_MODEL), 0.02),
        'w_in': normal(ks[4], (DEPTH, D_MODEL, N_IN), D_MODEL ** -0.5),
        'b_fox_f': jnp.linspace(1.0, 5.0, FOX_HEADS, dtype=jnp.float32)[None] + normal(ks[5], (DEPTH, FOX_HEADS), 0.1),
        'conv_w': normal(ks[6], (DEPTH, CONV_WIDTH, qk_cols), CONV_WIDTH ** -0.5),
        'conv_b': normal(ks[7], (DEPTH, qk_cols), 0.02),
        'b_mlstm_i': normal(ks[8], (DEPTH, MLSTM_HEADS), 0.1),
        'b_mlstm_f': jnp.linspace(3.0, 6.0, MLSTM_HEADS, dtype=jnp.float32)[None] + normal(ks[9], (DEPTH, MLSTM_HEADS), 0.1),
        'mlstm_head_norm': 1.0 + normal(ks[10], (DEPTH, MLSTM_WIDTH), 0.02),
        'w_a': normal(ks[11], (DEPTH, FOX_WIDTH, D_MODEL), FOX_WIDTH ** -0.5),
        'w_b': normal(ks[12], (DEPTH, MLSTM_WIDTH, D_MODEL), MLSTM_WIDTH ** -0.5),
        'w_o': normal(ks[13], (DEPTH, D_MODEL, D_MODEL), D_MODEL ** -0.5),
    }


def reference(x, meta_tokens, norm_pre, norm_post, w_in, b_fox_f, conv_w, conv_b,
              b_mlstm_i, b_mlstm_f, mlstm_head_norm, w_a, w_b, w_o):
    B = x.shape[0]
    meta = jnp.broadcast_to(meta_tokens[None].astype(x.dtype), (B, N_META, x.shape[-1]))
    h = jnp.concatenate([meta, x], axis=1)
    for l in range(DEPTH):
        h = hybrid_layer(h, norm_pre[l], norm_post[l], w_in[l], b_fox_f[l], conv_w[l], conv_b[l],
                         b_mlstm_i[l], b_mlstm_f[l], mlstm_head_norm[l], w_a[l], w_b[l], w_o[l])
    return h[:, N_META:]
```

```python
import os
import numpy as np
import ml_dtypes
from contextlib import ExitStack
import concourse.bass as bass
import concourse.mybir as mybir
from concourse.bass_utils import run_bass_kernel_spmd

F32 = mybir.dt.float32
BF16 = mybir.dt.bfloat16
AF = mybir.ActivationFunctionType
ALU = mybir.AluOpType

D = 1024
DEPTH = 4
NMETA = 16
SEQ = 4096
NIN = 14352
GT = 2
KSEG = 4
NKS = 6
KDIST = 4
NW = 4
C_FQ, C_FK, C_FV, C_FF, C_FZ = 0, 1024, 2048, 3072, 3080
C_MQK, C_MV, C_MI, C_MF, C_MO, C_MZ, C_GA, C_GB = 4104, 6152, 8200, 8204, 8208, 10256, 12304, 13328
EPS = 1e-6


class Op:
    __slots__ = ("eng", "fn", "deps", "dma_key", "ndma", "target", "flag", "is_dma")


class Prog:
    ENGS = ("pe", "act", "dve", "pool", "sp")

    def __init__(self):
        self.ops = []
        self.last_w = {}
        self.readers = {}
        self.last_dma = {}
        self.dma_count = {}

    def add(self, eng, fn, reads=(), writes=(), dma_key=None, ndma=1):
        op = Op()
        op.eng = eng
        op.fn = fn
        op.dma_key = dma_key
        op.is_dma = dma_key is not None
        op.ndma = ndma
        op.flag = 0
        op.target = 0
        deps = set()
        for r in reads:
            w = self.last_w.get(r)
            if w is not None:
                deps.add(w)
        for w_ in writes:
            lw = self.last_w.get(w_)
            if lw is not None:
                deps.add(lw)
            rd = self.readers.get(w_)
            if rd:
                deps.update(rd.values())
        for r in reads:
            d = self.readers.setdefault(r, {})
            d[("dma", dma_key) if op.is_dma else eng] = op
        for w_ in writes:
            self.last_w[w_] = op
            self.readers[w_] = {}
        if op.is_dma:
            prev = self.last_dma.get(dma_key)
            if prev is not None:
                deps.add(prev)
            self.last_dma[dma_key] = op
            n = self.dma_count.get(dma_key, 0) + ndma
            self.dma_count[dma_key] = n
            op.target = 16 * n
        deps.discard(op)
        op.deps = deps
        self.ops.append(op)
        return op

    def emit(self, nc, stack, final_keys):
        need_flag = set()
        for op in self.ops:
            for d in op.deps:
                if d.is_dma:
                    continue
                if d.eng == "pe" and op.eng == "pe" and not op.is_dma:
                    continue
                need_flag.add(d)
        cnt = {e: 0 for e in self.ENGS}
        for op in self.ops:
            if (not op.is_dma) and op in need_flag:
                cnt[op.eng] += 1
                op.flag = cnt[op.eng]
        esem = {e: stack.enter_context(nc.semaphore("s_" + e)) for e in self.ENGS}
        dsem = {}
        for i, k in enumerate(self.dma_count):
            dsem[k] = stack.enter_context(nc.semaphore("d%d" % i))
        per_eng = {e: [] for e in self.ENGS}
        for op in self.ops:
            per_eng[op.eng].append(op)
        block = stack.enter_context(nc.Block())
        final = [(dsem[k], 16 * self.dma_count[k]) for k in final_keys]

        def run(eng_name, e):
            waited = {}
            for op in per_eng[eng_name]:
                waits = {}
                for d in op.deps:
                    if d.is_dma:
                        s, v = dsem[d.dma_key], d.target
                    else:
                        if d.eng == "pe" and eng_name == "pe" and not op.is_dma:
                            continue
                        s, v = esem[d.eng], d.flag
                    if waits.get(s, 0) < v:
                        waits[s] = v
                for s, v in waits.items():
                    if waited.get(s, 0) < v:
                        e.wait_ge(s, v)
                        waited[s] = v
                r = op.fn(e)
                if op.is_dma:
                    if not isinstance(r, (list, tuple)):
                        r = [r]
                    assert len(r) == op.ndma
                    for ins in r:
                        ins.then_inc(dsem[op.dma_key], 16)
                elif op.flag:
                    r.then_inc(esem[eng_name], 1)
            if eng_name == "sp":
                for s, v in final:
                    e.wait_ge(s, v)

        @block.tensor
        def _(e):
            run("pe", e)

        @block.scalar
        def _(e):
            run("act", e)

        @block.vector
        def _(e):
            run("dve", e)

        @block.gpsimd
        def _(e):
            run("pool", e)

        @block.sync
        def _(e):
            run("sp", e)


def build_program(ntiles, depth):
    NTOK = ntiles * 128
    groups = [list(range(t, min(t + GT, ntiles))) for t in range(0, ntiles, GT)]
    NTM = GT * 128
    nc = bass.Bass("TRN2", target_bir_lowering=False)
    P = Prog()
    st = ExitStack()

    def dram(name, shape, dt, kind="Internal"):
        return nc.dram_tensor(name, shape, dt, kind=kind).ap()

    x_in = dram("x", [NTOK, D], F32, "ExternalInput")
    gpre_d = dram("gpre", [depth, 128, D], F32, "ExternalInput")
    gpost_d = dram("gpost", [depth, 128, D], F32, "ExternalInput")
    small_d = dram("small", [128, depth, 112], F32, "ExternalInput")
    w_in_d = dram("w_in", [depth, D, NIN], F32, "ExternalInput")
    w_a_d = dram("w_a", [depth, D, D], F32, "ExternalInput")
    w_b_d = dram("w_b", [depth, 2 * D, D], F32, "ExternalInput")
    w_o_d = dram("w_o", [depth, D, D], F32, "ExternalInput")
    out_d = dram("out", [NTOK, D], F32, "ExternalOutput")
    Hs = dram("Hs", [NTOK, D], F32)
    KTd = dram("KTd", [8, 128, NTOK], BF16)
    Vd = dram("Vd", [8, 128, ntiles, 128], BF16)
    Wb = dram("Wb", [depth, 39, 128, 8, 512], BF16)

    def sb(name, shape, dt):
        return st.enter_context(nc.sbuf_tensor("sb_" + name, shape, dt))

    def ps(name, shape, dt):
        return st.enter_context(nc.psum_tensor("ps_" + name, shape, dt))

    ident = sb("ident", [128, 128], BF16)
    mask01 = sb("mask01", [128, 128], BF16)
    maskneg = sb("maskneg", [128, 128], BF16)
    ones_bf = sb("ones_bf", [128, 128], BF16)
    tri_f = sb("tri_f", [128, 128], F32)
    ones_f = sb("ones_f", [128, 128], F32)
    cst = sb("cst", [128, 4], F32)
    mhalf = sb("mhalf", [128, 8], F32)
    small = sb("small", [128, depth, 112], F32)
    gpre = sb("gpre_t", [128, D], F32)
    gpost = sb("gpost_t", [128, D], F32)
    xt = [sb("xt%d" % i, [128, D], F32) for i in range(2)]
    junk = sb("junk", [128, D], BF16)
    ssA = sb("ssA", [128, 8], F32)
    xb = [sb("xb%d" % i, [128, D], BF16) for i in range(2)]
    hnTb = [sb("hnT%d" % i, [128, 8, NTM], BF16) for i in range(2)]
    Wsl = [sb("W%d" % i, [128, 8, 512], BF16) for i in range(NW)]
    vnew = sb("vnew", [128, GT, 8, 128], BF16)
    qT = [sb("qT%d" % i, [128, NTM], BF16) for i in range(2)]
    kTn = [sb("kTn%d" % i, [128, NTM], BF16) for i in range(2)]
    sfz = [sb("sfz%d" % i, [128, NTM], BF16) for i in range(2)]
    KTs = [sb("KTs%d" % i, [128, KSEG * 128], BF16) for i in range(NKS)]
    Vs = [sb("Vs%d" % i, [128, KSEG, 128], BF16) for i in range(NKS)]
    PT = [sb("PT%d" % i, [128, NTM], BF16) for i in range(4)]
    rl = [sb("rl%d" % i, [128, NTM], F32) for i in range(2)]
    Osb = [sb("Osb%d" % i, [128, NTM], F32) for i in range(2)]
    xaT = sb("xaT", [128, 8, NTM], BF16)
    Fcum = sb("Fcum", [128, ntiles, 8], F32)
    carry = sb("carry", [128, 8], F32)
    Fref = sb("Fref", [128, 8], F32)
    biasH = [sb("biasH%d" % i, [128, ntiles], F32) for i in range(2)]
    gz = sb("gz", [128, 16], F32)
    ge = sb("ge", [128, 16], F32)
    gsp = sb("gsp", [128, 16], F32)
    lg = sb("lg", [128, 12], F32)
    bloc = sb("bloc", [128, GT, 4], F32)
    blast = sb("blast", [128, GT, 4], F32)
    gtmp = sb("gtmp", [128, GT, 8], F32)
    gsT = sb("gsT", [128, GT, 4], F32)
    gdsT = sb("gdsT", [128, GT, 4], F32)
    ehT = sb("ehT", [128, GT, 4], F32)
    decT = sb("decT", [128, GT, 4], F32)
    stage = [sb("stage%d" % i, [128, NTM + 3], BF16) for i in range(3)]
    hist = sb("hist", [128, 16, 3], BF16)
    cdiag = sb("cdiag", [128, 64, 128], BF16)
    qkTb = [sb("qkT%d" % i, [128, 4, NTM], BF16) for i in range(2)]
    smz = [sb("smz%d" % i, [128, NTM], BF16) for i in range(2)]
    smzgb = [sb("smzg%d" % i, [128, 4, NTM], BF16) for i in range(2)]
    vmb = [sb("vm%d" % i, [128, GT, 512], BF16) for i in range(2)]
    sgob = [sb("sgo%d" % i, [128, GT, 512], BF16) for i in range(2)]
    ktok = [sb("ktok%d" % i, [128, 256], BF16) for i in range(2)]
    STm = [sb("STm%d" % i, [128, 128], BF16) for i in range(2)]
    dtmp = sb("dtmp", [128, 8], F32)
    hb = sb("hb", [128, GT, 512], F32)
    ssq = sb("ssq", [128, 8], F32)
    hbn = [sb("hbn%d" % i, [128, 512], BF16) for i in range(2)]
    hbT = sb("hbT", [128, 16, NTM], BF16)
    Cf = sb("Cf", [128, 8, 512], F32)
    Cb = sb("Cb", [128, 8, 512], BF16)
    nf = sb("nf", [128, 8], F32)
    nb = sb("nb", [128, 8], BF16)
    sga = sb("sga", [128, 8, NTM], BF16)
    sgb = sb("sgb", [128, 8, NTM], BF16)
    etmp = [sb("etmp%d" % i, [128, NTM], F32) for i in range(2)]
    etmpA = [sb("etmpA%d" % i, [128, NTM], F32) for i in range(4)]
    mrg = sb("mrg", [128, 8, NTM], BF16)
    otile = [sb("otile%d" % i, [128, D], F32) for i in range(2)]
    xres = [sb("xres%d" % i, [128, D], F32) for i in range(2)]
    ssE = sb("ssE", [128, 8], F32)
    pa = [ps("pa%d" % i, [128, 512], F32) for i in range(4)]
    pO = ps("pO", [128, 512], F32)
    pL = ps("pL", [128, 512], F32)
    ptr = ps("ptr", [128, 8, 128], BF16)
    psm = ps("psm", [128, 512], F32)

    rot = {}

    def nxt(key, n):
        v = rot.get(key, 0)
        rot[key] = v + 1
        return v % n

    def c_memset(t, val):
        P.add("pool", lambda e, t=t, val=val: e.memset(t[:], val), writes=[("c", id(t))])

    def c_select(t, fill, keep_ge):
        if keep_ge:
            P.add("pool", lambda e, t=t: e.affine_select(out=t[:], in_=t[:], pattern=[[1, 128]],
                                                        compare_op=ALU.is_ge, fill=fill, base=0,
                                                        channel_multiplier=-1),
                  reads=[("c", id(t))], writes=[("c", id(t))])
        else:
            P.add("pool", lambda e, t=t: e.affine_select(out=t[:], in_=t[:], pattern=[[-1, 128]],
                                                        compare_op=ALU.not_equal, fill=fill, base=0,
                                                        channel_multiplier=1),
                  reads=[("c", id(t))], writes=[("c", id(t))])

    c_memset(ident, 0.0)
    c_select(ident, 1.0, False)
    c_memset(mask01, 1.0)
    c_select(mask01, 0.0, True)
    c_memset(maskneg, 0.0)
    c_select(maskneg, -30000.0, True)
    c_memset(ones_bf, 1.0)
    c_memset(tri_f, 1.0)
    c_select(tri_f, 0.0, True)
    c_memset(ones_f, 1.0)
    P.add("pool", lambda e: e.memset(mhalf[:], -0.5), writes=[("c", "mhalf")])
    P.add("pool", lambda e: e.memset(cst[:, 0:1], EPS), writes=[("c", "cst0")])
    P.add("pool", lambda e: e.memset(cst[:, 1:2], 1.0), writes=[("c", "cst1")])
    P.add("pool", lambda e: e.memset(cst[:, 2:3], -float(np.log(16.0))), writes=[("c", "cst2")])
    CONST = [("c", id(ident)), ("c", id(mask01)), ("c", id(maskneg)), ("c", id(ones_bf)),
             ("c", id(tri_f)), ("c", id(ones_f)), ("c", "cst0"), ("c", "cst1"), ("c", "cst2"), ("c", "mhalf")]
    P.add("sp", lambda e: e.dma_start(out=small[:], in_=small_d[:, :, :]), writes=["small"],
          dma_key="small")

    wstate = {"units": [], "issued": 0, "cur_layer": 0, "cur_uidx": 0}
    NUNITS = 39

    def w_unit(parts):
        wstate["units"].append((wstate["cur_layer"], wstate["cur_uidx"], parts))
        wstate["cur_uidx"] += 1
        return len(wstate["units"]) - 1

    def w_issue_upto(u):
        while wstate["issued"] <= u and wstate["issued"] < len(wstate["units"]):
            i = wstate["issued"]
            slot = i % NW
            l_, uidx, parts = wstate["units"][i]
            if l_ == 0:
                def fn(e, slot=slot, parts=parts):
                    r = []
                    for (c0, ncols, src) in parts:
                        r.append(e.dma_start(out=Wsl[slot][:, :, c0:c0 + ncols],
                                             in_=src.rearrange("(k p) n -> p k n", p=128)))
                    return r
                P.add("pool", fn, writes=[("W", slot)], dma_key=("W", slot), ndma=len(parts))
            else:
                used = max(c0 + ncols for (c0, ncols, _) in parts)

                def fn(e, slot=slot, l_=l_, uidx=uidx, used=used):
                    return [e.dma_start(out=Wsl[slot][:, :, 0:used], in_=Wb[l_, uidx, :, :, 0:used])]
                P.add("pool", fn, reads=[("Wb", l_)], writes=[("W", slot)], dma_key=("W", slot), ndma=1)
            wstate["issued"] += 1

    def conv_chunks(l_, nchunks):
        first = next(i for i, (ll, ui, _) in enumerate(wstate["units"]) if ll == l_ and ui == 0)
        dmas = []
        for (ll, uidx, parts) in wstate["units"][first:first + NUNITS]:
            assert ll == l_
            for (c0, ncols, src) in parts:
                dmas.append((uidx, c0, ncols, src))
        per = (len(dmas) + nchunks - 1) // nchunks
        chunks = []
        for ci in range(nchunks):
            sub = dmas[ci * per:(ci + 1) * per]
            if not sub:
                chunks.append(None)
                continue

            def emit(sub=sub):
                def fn(e):
                    return [e.dma_start(out=Wb[l_, uidx, :, :, c0:c0 + ncols],
                                        in_=src.rearrange("(k p) n -> p k n", p=128))
                            for (uidx, c0, ncols, src) in sub]
                P.add("pool", fn, writes=[("Wb", l_)], dma_key=("wconv", l_), ndma=len(sub))
            chunks.append(emit)
        return chunks

    def w_get(u, first_live=None):
        fl = u if first_live is None else first_live
        w_issue_upto(max(u, fl + NW - 1))
        slot = u % NW
        return Wsl[slot], ("W", slot)

    unit_plan = []

    def plan_layer(l):
        pl = {}
        wi = w_in_d
        per_group = []
        for g in range(len(groups)):
            u = {}
            wstate["cur_layer"] = l
            wstate["cur_uidx"] = 0
            u["fv"] = [w_unit([(0, 512, wi[l, :, C_FV + j * 512:C_FV + (j + 1) * 512])]) for j in range(2)]
            u["gate"] = w_unit([(0, 8, wi[l, :, C_FF:C_FF + 8]), (8, 8, wi[l, :, C_MI:C_MI + 8])])
            u["fox"] = []
            for h in range(8):
                u["fox"].append(w_unit([(0, 128, wi[l, :, C_FQ + h * 128:C_FQ + (h + 1) * 128]),
                                        (128, 128, wi[l, :, C_FK + h * 128:C_FK + (h + 1) * 128]),
                                        (256, 128, wi[l, :, C_FZ + h * 128:C_FZ + (h + 1) * 128])]))
            u["ml"] = []
            for h in range(4):
                ua = w_unit([(0, 256, wi[l, :, C_MQK + h * 256:C_MQK + (h + 1) * 256]),
                             (256, 256, wi[l, :, C_MQK + 1024 + h * 256:C_MQK + 1024 + (h + 1) * 256])])
                ub = w_unit([(0, 512, wi[l, :, C_MZ + h * 512:C_MZ + (h + 1) * 512])])
                uc = w_unit([(0, 512, wi[l, :, C_MV + h * 512:C_MV + (h + 1) * 512])])
                ud = w_unit([(0, 512, wi[l, :, C_MO + h * 512:C_MO + (h + 1) * 512])])
                u["ml"].append((ua, ub, uc, ud))
            u["half"] = []
            for hf in range(2):
                uga = w_unit([(0, 512, wi[l, :, C_GA + hf * 512:C_GA + (hf + 1) * 512])])
                ugb = w_unit([(0, 512, wi[l, :, C_GB + hf * 512:C_GB + (hf + 1) * 512])])
                uwa = w_unit([(0, 512, w_a_d[l, :, hf * 512:(hf + 1) * 512])])
                uwb0 = w_unit([(0, 512, w_b_d[l, 0:1024, hf * 512:(hf + 1) * 512])])
                uwb1 = w_unit([(0, 512, w_b_d[l, 1024:2048, hf * 512:(hf + 1) * 512])])
                u["half"].append((uga, ugb, uwa, uwb0, uwb1))
            u["wo"] = [w_unit([(0, 512, w_o_d[l, :, j * 512:(j + 1) * 512])]) for j in range(2)]
            assert wstate["cur_uidx"] == NUNITS
            per_group.append(u)
        pl["groups"] = per_group
        return pl

    for l in range(depth):
        unit_plan.append(plan_layer(l))

    def proj_feat(W, wkey, c0, rhs_t, rhs_key, nk, NT, dst_ps, dst_key, koff=0, extra_reads=()):
        def fn(e):
            r = None
            for k in range(nk):
                r = e.matmul(dst_ps[:, 0:NT], lhsT=W[:, k, c0:c0 + 128], rhs=rhs_t[:, koff + k, 0:NT],
                             start=(k == 0), stop=(k == nk - 1))
            return r
        P.add("pe", fn, reads=[wkey, rhs_key] + list(extra_reads), writes=[dst_key])

    def proj_tok(W, wkey, ncols, ti, dst_ps, dst_key, hnT, HK):
        def fn(e):
            r = None
            for k in range(8):
                r = e.matmul(dst_ps[:, 0:ncols], lhsT=hnT[:, k, ti * 128:(ti + 1) * 128],
                             rhs=W[:, k, 0:ncols], start=(k == 0), stop=(k == 7))
            return r
        P.add("pe", fn, reads=[wkey, HK], writes=[dst_key])

    def pa_next():
        i = nxt("pa", 4)
        return pa[i], ("pa", i)

    def layer(l, H_in, hin_name, H_out, hout_name, par0):
        plan = unit_plan[l]
        sm = small[:, l, :]

        def pre_A():
            P.add("sp", lambda e: e.dma_start(out=gpre[:], in_=gpre_d[l, :, :]), writes=["gpre"], dma_key="gpre")

        def pre_body():
            P.add("sp", lambda e: e.dma_start(out=gpost[:], in_=gpost_d[l, :, :]), writes=["gpost"], dma_key="gpost")
            P.add("pool", lambda e: e.memset(Cf[:], 0.0), writes=[("Cf", i) for i in range(8)])
            P.add("pool", lambda e: e.memset(Cb[:], 0.0), writes=[("Cb", i) for i in range(8)])
            P.add("pool", lambda e: e.memset(nf[:], 0.0), writes=["nf"])
            P.add("pool", lambda e: e.memset(nb[:], 0.0), writes=["nb"])
            P.add("pool", lambda e: e.memset(hist[:], 0.0), writes=[("hist", i) for i in range(16)])
            P.add("pool", lambda e: e.memset(carry[:], 0.0), writes=["carry"])
            for j in range(64):
                P.add("dve", lambda e, j=j: e.tensor_scalar(out=cdiag[:, j, :], in0=ident[:], scalar1=sm[:, 16 + j:17 + j],
                                                          scalar2=None, op0=ALU.mult),
                      reads=["small"] + CONST, writes=["cdiag"])

        def make_group(g, tiles, par):
            U = plan["groups"][g]
            nt = len(tiles)
            NT = nt * 128
            T0 = tiles[0]
            tok0 = T0 * 128

            hnT = hnTb[par]
            HK = ("hnT", par)

            def gate_items(hf):
                uga, ugb = U["half"][hf][0], U["half"][hf][1]
                items = []

                def one(u, dst, dname, mm):
                    WG, wgk = w_get(u)
                    m = hf * 4 + mm
                    pc, pck = pa_next()
                    proj_feat(WG, wgk, mm * 128, hnT, HK, 8, NT, pc, pck)
                    P.add("act", lambda e: e.activation(out=dst[:, m, 0:NT], in_=pc[:, 0:NT], func=AF.Sigmoid),
                          reads=[pck], writes=[(dname, m)])
                for mm in range(4):
                    items.append(lambda mm=mm: one(uga, sga, "sga", mm))
                for mm in range(4):
                    items.append(lambda mm=mm: one(ugb, sgb, "sgb", mm))
                return items

            def stageA():
                for i, T in enumerate(tiles):
                    s = nxt("xt", 2)
                    P.add("sp", lambda e, s=s, T=T: e.dma_start(out=xt[s][:], in_=H_in[T * 128:(T + 1) * 128, :]),
                          reads=[(hin_name, T)], writes=[("xt", s)], dma_key=("xt", s))
                    P.add("act", lambda e, s=s, i=i: e.activation(out=junk[:], in_=xt[s][:], func=AF.Square,
                                                                accum_out=ssA[:, i:i + 1]),
                          reads=[("xt", s)], writes=["junk", ("ssA", i)])
                P.add("pool", lambda e: e.tensor_scalar(out=ssA[:, 4:4 + nt], in0=ssA[:, 0:0 + nt],
                                                        scalar1=1.0 / D, scalar2=EPS, op0=ALU.mult, op1=ALU.add),
                      reads=[("ssA", i) for i in range(nt)], writes=["ssA_s"])
                P.add("pool", lambda e: e.tensor_tensor(out=ssA[:, 4:4 + nt], in0=ssA[:, 4:4 + nt],
                                                        in1=mhalf[:, 0:nt], op=ALU.pow),
                      reads=["ssA_s"] + CONST, writes=["ssA_s"])
                base_slot = (rot["xt"] - nt) % 2
                for i, T in enumerate(tiles):
                    s = (base_slot + i) % 2
                    b = nxt("xb", 2)
                    P.add("dve", lambda e, s=s, b=b, i=i: e.scalar_tensor_tensor(
                        out=xb[b][:], in0=xt[s][:], scalar=ssA[:, 4 + i:5 + i], in1=gpre[:],
                        op0=ALU.mult, op1=ALU.mult),
                        reads=[("xt", s), "ssA_s", "gpre"], writes=[("xb", b)])

                    def fn(e, b=b):
                        r = None
                        for k in range(8):
                            r = e.transpose(out=ptr[:, k, :], in_=xb[b][:, k * 128:(k + 1) * 128], identity=ident[:])
                        return r
                    P.add("pe", fn, reads=[("xb", b)] + CONST, writes=["ptr"])
                    P.add("dve", lambda e, i=i: e.tensor_copy(out=hnT[:, :, i * 128:(i + 1) * 128], in_=ptr[:, :, :]),
                          reads=["ptr"], writes=[HK])


            def body():
                for j in range(2):
                    W, wk = w_get(U["fv"][j])
                    for i, T in enumerate(tiles):
                        pt_, pk = pa_next()
                        proj_tok(W, wk, 512, i, pt_, pk, hnT, HK)
                        P.add("act", lambda e, i=i, j=j, pt_=pt_: e.copy(
                            out=vnew[:, i, j * 4:(j + 1) * 4, :], in_=pt_[:, :].rearrange("p (h d) -> p h d", h=4)),
                            reads=[pk], writes=[("vnew", i)])
                for i, T in enumerate(tiles):
                    P.add("sp", lambda e, i=i, T=T: e.dma_start(out=Vd[:, :, T, :].rearrange("h p d -> p h d"),
                                                              in_=vnew[:, i, :, :]),
                          reads=[("vnew", i)], writes=["Vd"], dma_key=("vst", i))
                W, wk = w_get(U["gate"])
                for i, T in enumerate(tiles):
                    proj_tok(W, wk, 16, i, psm, "psm", hnT, HK)
                    P.add("dve", lambda e: e.tensor_tensor(out=gz[:], in0=psm[:, 0:16], in1=sm[:, 0:16], op=ALU.add),
                          reads=["psm", "small"], writes=["gz"])
                    P.add("act", lambda e: e.activation(out=ge[:], in_=gz[:], func=AF.Exp, scale=-1.0),
                          reads=["gz"], writes=["ge"])
                    P.add("act", lambda e: e.activation(out=gsp[:], in_=ge[:], func=AF.Ln, bias=cst[:, 1:2]),
                          reads=["ge"] + CONST, writes=["gsp"])
                    P.add("dve", lambda e: e.tensor_scalar(out=lg[:, 0:8], in0=gsp[:, 0:8], scalar1=-1.0, scalar2=None,
                                                           op0=ALU.mult), reads=["gsp"], writes=["lg0"])
                    P.add("dve", lambda e: e.tensor_scalar(out=lg[:, 8:12], in0=gsp[:, 12:16], scalar1=-1.0,
                                                           scalar2=None, op0=ALU.mult), reads=["gsp"], writes=["lg1"])
                    P.add("pe", lambda e: e.matmul(psm[:, 32:44], lhsT=tri_f[:], rhs=lg[:], start=True, stop=True),
                          reads=["lg0", "lg1"] + CONST, writes=["psm"])
                    P.add("dve", lambda e, T=T: e.tensor_tensor(out=Fcum[:, T, :], in0=psm[:, 32:40], in1=carry[:],
                                                               op=ALU.add),
                          reads=["psm", "carry"], writes=["Fcum"])
                    P.add("dve", lambda e, i=i: e.tensor_copy(out=bloc[:, i, :], in_=psm[:, 40:44]),
                          reads=["psm"], writes=[("bloc", i)])
                    P.add("pe", lambda e: e.matmul(psm[:, 64:76], lhsT=ones_f[:], rhs=lg[:], start=True, stop=True),
                          reads=["lg0", "lg1", "Fcum", ("bloc", i)] + CONST, writes=["psm"])
                    P.add("dve", lambda e: e.tensor_tensor(out=carry[:], in0=psm[:, 64:72], in1=carry[:], op=ALU.add),
                          reads=["psm", "carry", "Fcum"], writes=["carry"])
                    P.add("dve", lambda e, i=i: e.tensor_copy(out=blast[:, i, :], in_=psm[:, 72:76]),
                          reads=["psm"], writes=[("blast", i)])
                    if i == 0:
                        P.add("dve", lambda e: e.tensor_copy(out=Fref[:], in_=carry[:]), reads=["carry"], writes=["Fref"])
                    P.add("dve", lambda e, i=i: e.tensor_tensor(out=gtmp[:, i, 0:4], in0=gz[:, 8:12], in1=bloc[:, i, :],
                                                               op=ALU.subtract),
                          reads=["gz", ("bloc", i)], writes=[("gtmp0", i)])
                    P.add("dve", lambda e, i=i: e.tensor_tensor(out=gtmp[:, i, 4:8], in0=gtmp[:, i, 0:4],
                                                               in1=blast[:, i, :], op=ALU.add),
                          reads=[("gtmp0", i), ("blast", i)], writes=[("gtmp1", i)])
                    P.add("act", lambda e, i=i: e.activation(out=gsT[:, i, :], in_=gtmp[:, i, 0:4], func=AF.Exp),
                          reads=[("gtmp0", i)], writes=[("gsT", i)])
                    P.add("act", lambda e, i=i: e.activation(out=gdsT[:, i, :], in_=gtmp[:, i, 4:8], func=AF.Exp),
                          reads=[("gtmp1", i)], writes=[("gdsT", i)])
                    P.add("act", lambda e, i=i: e.activation(out=ehT[:, i, :], in_=bloc[:, i, :], func=AF.Exp,
                                                             bias=cst[:, 2:3]),
                          reads=[("bloc", i)] + CONST, writes=[("ehT", i)])
                    P.add("act", lambda e, i=i: e.activation(out=decT[:, i, :], in_=blast[:, i, :], func=AF.Exp),
                          reads=[("blast", i)], writes=[("decT", i)])


                nkb = tiles[-1] + 1

                def fox_proj(h):
                    W, wk = w_get(U["fox"][h])
                    hb_ = h % 2
                    pq, pqk = pa_next()
                    proj_feat(W, wk, 0, hnT, HK, 8, NT, pq, pqk)
                    P.add("dve", lambda e, hb_=hb_, pq=pq: e.tensor_scalar(
                        out=qT[hb_][:, 0:NT], in0=pq[:, 0:NT], scalar1=float(128 ** -0.5), scalar2=None, op0=ALU.mult),
                        reads=[pqk], writes=[("qT", hb_)])
                    pk_, pkk = pa_next()
                    proj_feat(W, wk, 128, hnT, HK, 8, NT, pk_, pkk)
                    P.add("dve", lambda e, hb_=hb_, pk_=pk_: e.tensor_copy(out=kTn[hb_][:, 0:NT], in_=pk_[:, 0:NT]),
                          reads=[pkk], writes=[("kTn", hb_)])
                    P.add("sp", lambda e, hb_=hb_, h=h: e.dma_start(
                        out=KTd[h, :, tok0:tok0 + NT], in_=kTn[hb_][:, 0:NT]),
                        reads=[("kTn", hb_)], writes=[("KTd", h)], dma_key=("kst", hb_))
                    pz, pzk = pa_next()
                    proj_feat(W, wk, 256, hnT, HK, 8, NT, pz, pzk)
                    P.add("act", lambda e, hb_=hb_, pz=pz: e.activation(out=sfz[hb_][:, 0:NT], in_=pz[:, 0:NT],
                                                                       func=AF.Tanh, scale=0.5),
                          reads=[pzk], writes=[("sfz", hb_)])
                    P.add("dve", lambda e, hb_=hb_, pz=pz: e.scalar_tensor_tensor(
                        out=sfz[hb_][:, 0:NT], in0=sfz[hb_][:, 0:NT], scalar=1.0, in1=pz[:, 0:NT],
                        op0=ALU.add, op1=ALU.mult),
                        reads=[pzk, ("sfz", hb_)], writes=[("sfz", hb_)])
                    P.add("dve", lambda e, hb_=hb_, h=h: e.tensor_scalar(
                        out=biasH[hb_][:, 0:nkb], in0=Fcum[:, 0:nkb, h], scalar1=-1.0, scalar2=Fref[:, h:h + 1],
                        op0=ALU.mult, op1=ALU.add),
                        reads=["Fcum", "Fref"], writes=[("biasH", hb_)])

                seg_slot = {}

                def fox_load(h, si):
                    s0 = si * KSEG
                    nb_ = min(KSEG, nkb - s0)
                    ks = nxt("kseg", NKS)
                    seg_slot[(h, si)] = ks
                    P.add("sp", lambda e: e.dma_start(
                        out=KTs[ks][:, 0:nb_ * 128], in_=KTd[h, :, s0 * 128:(s0 + nb_) * 128]),
                        reads=[("KTd", h)], writes=[("KTs", ks)], dma_key=("KTs", ks))
                    P.add("sp", lambda e: e.dma_start(
                        out=Vs[ks][:, 0:nb_, :], in_=Vd[h, :, s0:s0 + nb_, :]),
                        reads=["Vd"], writes=[("Vs", ks)], dma_key=("Vs", ks))

                nseg = (nkb + KSEG - 1) // KSEG
                load_list = [(hh, si) for hh in range(8) for si in range(nseg)]
                load_state = {"n": 0}

                def ensure_loads(h, si, maxh=None):
                    upto = load_list.index((h, si)) + KDIST
                    while load_state["n"] < len(load_list) and load_state["n"] <= upto:
                        hh, s2 = load_list[load_state["n"]]
                        if hh > (h + 1 if maxh is None else maxh):
                            break
                        fox_load(hh, s2)
                        load_state["n"] += 1

                def fox_attn(h):
                    hb_ = h % 2
                    blocks = []
                    for si in range(nseg):
                        s0 = si * KSEG
                        nb_ = min(KSEG, nkb - s0)
                        for jj in range(nb_):
                            blocks.append((si, jj, s0 + jj))
                    st_ = {}

                    def qk_exp(n):
                        si, jj, j = blocks[n]
                        if jj == 0:
                            ensure_loads(h, si)
                        ks = seg_slot[(h, si)]
                        diag = j >= T0
                        q0 = (j - T0) * 128 if diag else 0
                        pS, pSk = pa_next()
                        p_ = nxt("PT", 4)
                        st_[n] = (p_, q0)

                        def fn(e):
                            r = e.matmul(pS[:, q0:NT], lhsT=KTs[ks][:, jj * 128:(jj + 1) * 128],
                                         rhs=qT[hb_][:, q0:NT], start=True, stop=(not diag))
                            if diag:
                                r = e.matmul(pS[:, q0:q0 + 128], lhsT=ident[:], rhs=maskneg[:],
                                             start=False, stop=True)
                            return r
                        P.add("pe", fn, reads=[("KTs", ks), ("qT", hb_)] + CONST, writes=[pSk])
                        P.add("act", lambda e: e.activation(
                            out=PT[p_][:, q0:NT], in_=pS[:, q0:NT], func=AF.Exp, bias=biasH[hb_][:, j:j + 1]),
                            reads=[pSk, ("biasH", hb_)], writes=[("PT", p_)])

                    def pv(n):
                        si, jj, j = blocks[n]
                        ks = seg_slot[(h, si)]
                        p_, q0 = st_[n]

                        def fn2(e):
                            e.matmul(pO[:, q0:NT], lhsT=Vs[ks][:, jj, :], rhs=PT[p_][:, q0:NT],
                                     start=(j == 0), stop=(j == nkb - 1))
                            return e.matmul(pL[:, q0:NT], lhsT=ones_bf[:], rhs=PT[p_][:, q0:NT],
                                            start=(j == 0), stop=(j == nkb - 1))
                        P.add("pe", fn2, reads=[("Vs", ks), ("PT", p_)] + CONST, writes=["pO", "pL"])

                    nblk = len(blocks)
                    AHEAD = 3
                    for n in range(min(AHEAD, nblk)):
                        qk_exp(n)
                    for n in range(nblk):
                        pv(n)
                        if n + AHEAD < nblk:
                            qk_exp(n + AHEAD)
                    r_ = nxt("rl", 2)
                    P.add("dve", lambda e: e.tensor_copy(out=rl[r_][:, 0:NT], in_=pL[:, 0:NT]),
                          reads=["pL"], writes=[("rl", r_)])
                    P.add("dve", lambda e: e.tensor_copy(out=Osb[r_][:, 0:NT], in_=pO[:, 0:NT]),
                          reads=["pO"], writes=[("Osb", r_)])
                    P.add("dve", lambda e: e.reciprocal(out=rl[r_][:, 0:NT], in_=rl[r_][:, 0:NT]),
                          reads=[("rl", r_)], writes=[("rl", r_)])
                    P.add("dve", lambda e: e.scalar_tensor_tensor(
                        out=rl[r_][:, 0:NT], in0=rl[r_][:, 0:NT], scalar=0.5, in1=sfz[hb_][:, 0:NT],
                        op0=ALU.mult, op1=ALU.mult),
                        reads=[("rl", r_), ("sfz", hb_)], writes=[("rl", r_)])
                    P.add("dve", lambda e: e.tensor_tensor(
                        out=xaT[:, h, 0:NT], in0=Osb[r_][:, 0:NT], in1=rl[r_][:, 0:NT], op=ALU.mult),
                        reads=[("Osb", r_), ("rl", r_)], writes=["xaT"])


                def ml_items(h):
                    s_ = h % 2
                    ua, ub, uc, ud = U["ml"][h]
                    qk_, smzg_, vm_, sgo_ = qkTb[s_], smzgb[s_], vmb[s_], sgob[s_]

                    cst_ = {}

                    def conv_p(c):
                        WA, wak = w_get(ua)
                        ch = (2 * h + c) if c < 2 else (8 + 2 * h + (c - 2))
                        pc, pck = pa_next()
                        proj_feat(WA, wak, c * 128, hnT, HK, 8, NT, pc, pck)
                        sg = nxt("stage", 3)
                        cst_[c] = (sg, ch)
                        P.add("act", lambda e: e.copy(out=stage[sg][:, 3:3 + NT], in_=pc[:, 0:NT]),
                              reads=[pck], writes=[("stage", sg)])
                        P.add("dve", lambda e: e.tensor_copy(out=stage[sg][:, 0:3], in_=hist[:, ch, :]),
                              reads=[("hist", ch)], writes=[("stageh", sg)])

                    def conv_c(c):
                        sg, ch = cst_[c]
                        pv_, pvk = pa_next()

                        def fn(e):
                            r = None
                            for tp in range(4):
                                r = e.matmul(pv_[:, 0:NT], lhsT=cdiag[:, ch * 4 + tp, :], rhs=stage[sg][:, tp:tp + NT],
                                             start=(tp == 0), stop=(tp == 3))
                            return r
                        P.add("pe", fn, reads=[("stage", sg), ("stageh", sg), "cdiag"], writes=[pvk])
                        P.add("dve", lambda e: e.tensor_copy(out=hist[:, ch, :], in_=stage[sg][:, NT:NT + 3]),
                              reads=[("stage", sg), ("stageh", sg)], writes=[("hist", ch)])
                        P.add("act", lambda e: e.activation(out=qk_[:, c, 0:NT], in_=pv_[:, 0:NT], func=AF.Silu,
                                                            bias=sm[:, 80 + ch:81 + ch]),
                              reads=[pvk, "small"], writes=[("qkT", s_, c)])

                    def mz_chunk(c):
                        WB, wbk = w_get(ub)
                        pc, pck = pa_next()
                        proj_feat(WB, wbk, c * 128, hnT, HK, 8, NT, pc, pck)
                        z_ = nxt("smz", 2)
                        P.add("act", lambda e: e.activation(out=smz[z_][:, 0:NT], in_=pc[:, 0:NT], func=AF.Silu),
                              reads=[pck], writes=[("smz", z_)])
                        P.add("dve", lambda e: e.tensor_scalar(
                            out=smzg_[:, c, 0:NT], in0=smz[z_][:, 0:NT], scalar1=sm[:, 96 + h * 4 + c:97 + h * 4 + c],
                            scalar2=None, op0=ALU.mult),
                            reads=[("smz", z_), "small"], writes=[("smzg", s_, c)])

                    def v_tile(i):
                        WC, wck = w_get(uc)
                        pc, pck = pa_next()
                        proj_tok(WC, wck, 512, i, pc, pck, hnT, HK)
                        P.add("act", lambda e: e.copy(out=vm_[:, i, :], in_=pc[:, :]), reads=[pck], writes=[("vm", s_, i)])

                    def o_tile(i):
                        WD, wdk = w_get(ud)
                        pc, pck = pa_next()
                        proj_tok(WD, wdk, 512, i, pc, pck, hnT, HK)
                        P.add("act", lambda e: e.activation(out=sgo_[:, i, :], in_=pc[:, :], func=AF.Sigmoid),
                              reads=[pck], writes=[("sgo", s_, i)])

                    items = []
                    for (kind, c) in (("p", 0), ("p", 1), ("c", 0), ("p", 2), ("c", 1), ("p", 3), ("c", 2), ("c", 3)):
                        items.append((lambda c=c: conv_p(c)) if kind == "p" else (lambda c=c: conv_c(c)))
                    for c in range(4):
                        items.append(lambda c=c: mz_chunk(c))
                    for i in range(nt):
                        items.append(lambda i=i: v_tile(i))
                    for i in range(nt):
                        items.append(lambda i=i: o_tile(i))
                    return items

                def ml_head(h, filler):
                    s_ = h % 2
                    qk_, smzg_, vm_, sgo_ = qkTb[s_], smzgb[s_], vmb[s_], sgob[s_]
                    QK = [("qkT", s_, c) for c in range(4)]

                    per_tile = (len(filler) * 3) // (4 * nt) if nt else 0

                    def fill(n):
                        while n > 0 and filler:
                            filler.pop(0)()
                            n -= 1

                    for i in range(nt):
                        cs = slice(i * 128, (i + 1) * 128)
                        kt = nxt("ktok", 2)
                        sT = nxt("STm", 2)

                        def fn(e, cs=cs):
                            e.transpose(out=ptr[:, 0, :], in_=qk_[:, 2, cs], identity=ident[:])
                            return e.transpose(out=ptr[:, 1, :], in_=qk_[:, 3, cs], identity=ident[:])
                        P.add("pe", fn, reads=[QK[2], QK[3]] + CONST, writes=["ptr"])
                        P.add("dve", lambda e, kt=kt, i=i: e.tensor_scalar(
                            out=ktok[kt][:, :], in0=ptr[:, 0:2, :].rearrange("p a b -> p (a b)"),
                            scalar1=gdsT[:, i, h:h + 1], scalar2=None, op0=ALU.mult),
                            reads=["ptr", ("gdsT", i)], writes=[("ktok", kt)])

                        def fn(e, cs=cs):
                            e.matmul(pO[:, 0:128], lhsT=qk_[:, 2, cs], rhs=qk_[:, 0, cs], start=True, stop=False)
                            return e.matmul(pO[:, 0:128], lhsT=qk_[:, 3, cs], rhs=qk_[:, 1, cs], start=False, stop=True)
                        P.add("pe", fn, reads=QK, writes=["pO"])
                        P.add("dve", lambda e, sT=sT, i=i: e.scalar_tensor_tensor(
                            out=STm[sT][:], in0=pO[:, 0:128], scalar=gsT[:, i, h:h + 1], in1=mask01[:],
                            op0=ALU.mult, op1=ALU.mult),
                            reads=["pO", ("gsT", i)] + CONST, writes=[("STm", sT)])
                        pn, pnk = pa_next()

                        def fn(e, sT=sT, i=i, cs=cs, pn=pn):
                            e.matmul(pn[:, :], lhsT=STm[sT][:], rhs=vm_[:, i, :], start=True, stop=False)
                            e.matmul(pn[:, :], lhsT=qk_[:, 0, cs], rhs=Cb[:, 2 * h, :], start=False, stop=False)
                            return e.matmul(pn[:, :], lhsT=qk_[:, 1, cs], rhs=Cb[:, 2 * h + 1, :], start=False, stop=True)
                        P.add("pe", fn, reads=[("STm", sT), ("vm", s_, i), QK[0], QK[1], ("Cb", 2 * h),
                                               ("Cb", 2 * h + 1)], writes=[pnk])

                        def fn(e, sT=sT, cs=cs):
                            e.matmul(psm[:, 128:129], lhsT=STm[sT][:], rhs=ones_bf[:, 0:1], start=True, stop=False)
                            e.matmul(psm[:, 128:129], lhsT=qk_[:, 0, cs], rhs=nb[:, 2 * h:2 * h + 1], start=False, stop=False)
                            return e.matmul(psm[:, 128:129], lhsT=qk_[:, 1, cs], rhs=nb[:, 2 * h + 1:2 * h + 2],
                                            start=False, stop=True)
                        P.add("pe", fn, reads=[("STm", sT), QK[0], QK[1], "nb"] + CONST, writes=["psm"])
                        P.add("act", lambda e, i=i: e.activation(
                            out=dtmp[:, 0:1], in_=psm[:, 128:129], func=AF.Abs, scale=ehT[:, i, h:h + 1]),
                            reads=["psm", ("ehT", i)], writes=["dtmp0"])
                        P.add("dve", lambda e: e.tensor_scalar(out=dtmp[:, 3:4], in0=dtmp[:, 0:1], scalar1=1.0,
                                                               scalar2=None, op0=ALU.max),
                              reads=["dtmp0"], writes=["dtmp3"])
                        P.add("dve", lambda e: e.reciprocal(out=dtmp[:, 1:2], in_=dtmp[:, 3:4]),
                              reads=["dtmp3"], writes=["dtmp1"])
                        P.add("dve", lambda e, i=i: e.tensor_tensor(out=dtmp[:, 2:3], in0=dtmp[:, 1:2],
                                                                   in1=ehT[:, i, h:h + 1], op=ALU.mult),
                              reads=["dtmp1", ("ehT", i)], writes=["dtmp2"])
                        P.add("dve", lambda e, i=i, pn=pn: e.scalar_tensor_tensor(
                            out=hb[:, i, :], in0=pn[:, :], scalar=dtmp[:, 2:3], in1=sgo_[:, i, :],
                            op0=ALU.mult, op1=ALU.mult),
                            reads=[pnk, "dtmp2", ("sgo", s_, i)], writes=[("hb", i)])
                        P.add("act", lambda e, i=i: e.activation(out=junk[:, 0:512], in_=hb[:, i, :], func=AF.Square,
                                                                 accum_out=ssq[:, i:i + 1]),
                              reads=[("hb", i)], writes=["junk", ("ssq", i)])
                        for c in range(2):
                            pd, pdk = pa_next()
                            P.add("pe", lambda e, kt=kt, c=c, i=i, pd=pd: e.matmul(
                                pd[:, :], lhsT=ktok[kt][:, c * 128:(c + 1) * 128], rhs=vm_[:, i, :], start=True, stop=True),
                                reads=[("ktok", kt), ("vm", s_, i)], writes=[pdk])
                            P.add("dve", lambda e, c=c, i=i, pd=pd: e.scalar_tensor_tensor(
                                out=Cf[:, 2 * h + c, :], in0=Cf[:, 2 * h + c, :], scalar=decT[:, i, h:h + 1],
                                in1=pd[:, :], op0=ALU.mult, op1=ALU.add),
                                reads=[pdk, ("Cf", 2 * h + c), ("decT", i)], writes=[("Cf", 2 * h + c)])
                            P.add("act", lambda e, c=c: e.copy(out=Cb[:, 2 * h + c, :], in_=Cf[:, 2 * h + c, :]),
                                  reads=[("Cf", 2 * h + c)], writes=[("Cb", 2 * h + c)])

                        def fn(e, kt=kt):
                            e.matmul(psm[:, 192:193], lhsT=ktok[kt][:, 0:128], rhs=ones_bf[:, 0:1], start=True, stop=True)
                            return e.matmul(psm[:, 193:194], lhsT=ktok[kt][:, 128:256], rhs=ones_bf[:, 0:1],
                                            start=True, stop=True)
                        P.add("pe", fn, reads=[("ktok", kt), "dtmp0"] + CONST, writes=["psm"])
                        P.add("dve", lambda e, i=i: e.scalar_tensor_tensor(
                            out=nf[:, 2 * h:2 * h + 2], in0=nf[:, 2 * h:2 * h + 2], scalar=decT[:, i, h:h + 1],
                            in1=psm[:, 192:194], op0=ALU.mult, op1=ALU.add),
                            reads=["psm", "nf", ("decT", i)], writes=["nf"])
                        P.add("dve", lambda e: e.tensor_copy(out=nb[:, 2 * h:2 * h + 2], in_=nf[:, 2 * h:2 * h + 2]),
                              reads=["nf"], writes=["nb"])
                        fill(per_tile)
                    P.add("pool", lambda e: e.tensor_scalar(out=ssq[:, 4:4 + nt], in0=ssq[:, 0:0 + nt],
                                                            scalar1=1.0 / 512, scalar2=EPS, op0=ALU.mult, op1=ALU.add),
                          reads=[("ssq", i) for i in range(nt)], writes=["ssq_s"])
                    P.add("pool", lambda e: e.tensor_tensor(out=ssq[:, 4:4 + nt], in0=ssq[:, 4:4 + nt],
                                                            in1=mhalf[:, 0:nt], op=ALU.pow),
                          reads=["ssq_s"] + CONST, writes=["ssq_s"])
                    hns = []
                    for i in range(nt):
                        hn_ = nxt("hbn", 2)
                        hns.append(hn_)
                        P.add("act", lambda e, hn_=hn_, i=i: e.activation(out=hbn[hn_][:], in_=hb[:, i, :], func=AF.Copy,
                                                                       scale=ssq[:, 4 + i:5 + i]),
                              reads=[("hb", i), "ssq_s"], writes=[("hbn", hn_)])
                    fill(10 ** 6)
                    for i in range(nt):
                        hn_ = hns[i]

                        def fn(e, hn_=hn_):
                            r = None
                            for c in range(4):
                                r = e.transpose(out=ptr[:, 4 + c, :], in_=hbn[hn_][:, c * 128:(c + 1) * 128],
                                                identity=ident[:])
                            return r
                        P.add("pe", fn, reads=[("hbn", hn_)] + CONST, writes=["ptr"])
                        P.add("dve", lambda e, i=i: e.tensor_tensor(
                            out=hbT[:, 4 * h:4 * h + 4, i * 128:(i + 1) * 128], in0=ptr[:, 4:8, :],
                            in1=smzg_[:, 0:4, i * 128:(i + 1) * 128], op=ALU.mult),
                            reads=["ptr"] + [("smzg", s_, c) for c in range(4)], writes=["hbT"])

                fox_proj(0)
                ensure_loads(0, 0, 0)
                for h in range(8):
                    if h + 1 < 8:
                        fox_proj(h + 1)
                    if h == 7:
                        for it in ml_items(0):
                            it()
                    fox_attn(h)
                for h in range(4):
                    filler = ml_items(h + 1) if h < 3 else gate_items(0)
                    ml_head(h, filler)


            def stageE():
                for hf in range(2):
                    uga, ugb, uwa, uwb0, uwb1 = U["half"][hf]
                    if hf == 1:
                        for it in gate_items(1):
                            it()
                    WAa, wak = w_get(uwa)
                    WB0, wb0k = w_get(uwb0, uwa)
                    WB1, wb1k = w_get(uwb1, uwa)
                    for mm in range(4):
                        m = hf * 4 + mm
                        pya, pyak = pa_next()
                        proj_feat(WAa, wak, mm * 128, xaT, "xaT", 8, NT, pya, pyak)
                        P.add("dve", lambda e, mm=mm, pya=pya, m=m: e.tensor_tensor(
                            out=etmpA[mm][:, 0:NT], in0=pya[:, 0:NT], in1=sga[:, m, 0:NT], op=ALU.mult),
                            reads=[pyak, ("sga", m)], writes=[("etmpA", mm)])
                    for mm in range(4):
                        m = hf * 4 + mm
                        pyb, pybk = pa_next()

                        def fn(e, mm=mm, pyb=pyb, WB0=WB0, WB1=WB1):
                            r = None
                            for k in range(16):
                                Wk = WB0 if k < 8 else WB1
                                r = e.matmul(pyb[:, 0:NT], lhsT=Wk[:, k % 8, mm * 128:(mm + 1) * 128], rhs=hbT[:, k, 0:NT],
                                             start=(k == 0), stop=(k == 15))
                            return r
                        P.add("pe", fn, reads=[wb0k, wb1k, "hbT"], writes=[pybk])
                        t1 = nxt("etmp", 2)
                        P.add("dve", lambda e, t1=t1, pyb=pyb, m=m: e.tensor_tensor(
                            out=etmp[t1][:, 0:NT], in0=pyb[:, 0:NT], in1=sgb[:, m, 0:NT], op=ALU.mult),
                            reads=[pybk, ("sgb", m)], writes=[("etmp", t1)])
                        P.add("dve", lambda e, mm=mm, t1=t1, m=m: e.tensor_tensor(
                            out=mrg[:, m, 0:NT], in0=etmpA[mm][:, 0:NT], in1=etmp[t1][:, 0:NT], op=ALU.add),
                            reads=[("etmpA", mm), ("etmp", t1)], writes=["mrg"])
                WO0, wo0k = w_get(U["wo"][0])
                WO1, wo1k = w_get(U["wo"][1], U["wo"][0])
                pouts = []
                for i, T in enumerate(tiles):
                    pp = []
                    for j, (WO, wok) in enumerate(((WO0, wo0k), (WO1, wo1k))):
                        po, pok = pa_next()

                        def fn(e, i=i, WO=WO, po=po):
                            r = None
                            for k in range(8):
                                r = e.matmul(po[:, :], lhsT=mrg[:, k, i * 128:(i + 1) * 128], rhs=WO[:, k, :],
                                             start=(k == 0), stop=(k == 7))
                            return r
                        P.add("pe", fn, reads=[wok, "mrg"], writes=[pok])
                        P.add("act", lambda e, po=po, i=i, j=j: e.activation(
                            out=junk[:, 0:512], in_=po[:, :], func=AF.Square, accum_out=ssE[:, 2 * i + j:2 * i + j + 1]),
                            reads=[pok], writes=["junk", ("ssE", 2 * i + j)])
                        pp.append((po, pok))
                    pouts.append(pp)
                    P.add("dve", lambda e, i=i: e.tensor_tensor(out=ssE[:, 4 + i:5 + i], in0=ssE[:, 2 * i:2 * i + 1],
                                                               in1=ssE[:, 2 * i + 1:2 * i + 2], op=ALU.add),
                          reads=[("ssE", 2 * i), ("ssE", 2 * i + 1)], writes=[("ssE_t", i)])
                P.add("pool", lambda e: e.tensor_scalar(out=ssE[:, 6:6 + nt], in0=ssE[:, 4:4 + nt],
                                                        scalar1=1.0 / D, scalar2=EPS, op0=ALU.mult, op1=ALU.add),
                      reads=[("ssE_t", i) for i in range(nt)], writes=["ssE_s"])
                P.add("pool", lambda e: e.tensor_tensor(out=ssE[:, 6:6 + nt], in0=ssE[:, 6:6 + nt],
                                                        in1=mhalf[:, 0:nt], op=ALU.pow),
                      reads=["ssE_s"] + CONST, writes=["ssE_s"])
                for i, T in enumerate(tiles):
                    xr = nxt("xres", 2)
                    ot = nxt("otile", 2)
                    P.add("sp", lambda e, xr=xr, T=T: e.dma_start(out=xres[xr][:], in_=H_in[T * 128:(T + 1) * 128, :]),
                          reads=[(hin_name, T)], writes=[("xres", xr)], dma_key=("xres", xr))
                    for j in range(2):
                        po, pok = pouts[i][j]
                        P.add("dve", lambda e, ot=ot, po=po, i=i, j=j: e.scalar_tensor_tensor(
                            out=otile[ot][:, j * 512:(j + 1) * 512], in0=po[:, :], scalar=ssE[:, 6 + i:7 + i],
                            in1=gpost[:, j * 512:(j + 1) * 512], op0=ALU.mult, op1=ALU.mult),
                            reads=[pok, "ssE_s", "gpost"], writes=[("otile", ot, j)])
                    P.add("dve", lambda e, ot=ot, xr=xr: e.tensor_tensor(out=otile[ot][:], in0=otile[ot][:],
                                                                         in1=xres[xr][:], op=ALU.add),
                          reads=[("otile", ot, 0), ("otile", ot, 1), ("xres", xr)],
                          writes=[("otile", ot, 0), ("otile", ot, 1)])
                    P.add("sp", lambda e, ot=ot, T=T: e.dma_start(out=H_out[T * 128:(T + 1) * 128, :], in_=otile[ot][:]),
                          reads=[("otile", ot, 0), ("otile", ot, 1)], writes=[(hout_name, T)], dma_key=("ost", ot))

            return stageA, body, stageE

        return pre_A, pre_body, [make_group(g, tiles, (par0 + g) % 2) for g, tiles in enumerate(groups)]

    layers = []
    par0 = 0
    for l in range(depth):
        H_in, hin = (x_in, "Hx") if l == 0 else (Hs, "Hs")
        H_out, hout = (out_d, "Hout") if l == depth - 1 else (Hs, "Hs")
        layers.append(layer(l, H_in, hin, H_out, hout, par0))
        par0 = (par0 + len(groups)) % 2
    steps = []
    for l in range(depth):
        pre_A, pre_body, gl = layers[l]
        cch = conv_chunks(l + 1, len(gl)) if l + 1 < depth else [None] * len(gl)
        for g, (sA, bd, sE) in enumerate(gl):
            def bd2(bd=bd, cc=cch[g]):
                if cc is not None:
                    cc()
                bd()
            steps.append((g, pre_A, pre_body, sA, bd2, sE))
    steps[0][1]()
    steps[0][3]()
    for idx, (g, pre_A, pre_body, sA, bd, sE) in enumerate(steps):
        if g == 0:
            pre_body()
        bd()
        if idx + 1 < len(steps):
            ng, npre_A, _, nsA, _, _ = steps[idx + 1]
            if ng == 0:
                npre_A()
            nsA()
        sE()

    P.emit(nc, st, final_keys=[("ost", 0), ("ost", 1)])
    st.close()
    return nc


def prep_inputs(x, meta_tokens, norm_pre, norm_post, w_in, b_fox_f, conv_w, conv_b, b_mlstm_i, b_mlstm_f,
                mlstm_head_norm, w_a, w_b, w_o, ntiles, depth):
    NTOK = ntiles * 128
    B = x.shape[0]
    f32 = np.float32
    gpre = np.ascontiguousarray(np.broadcast_to(np.asarray(norm_pre, f32)[:depth, None, :], (depth, 128, D)))
    gpost = np.ascontiguousarray(np.broadcast_to(np.asarray(norm_post, f32)[:depth, None, :], (depth, 128, D)))
    small = np.zeros((128, depth, 112), f32)
    for l in range(depth):
        gb = np.concatenate([np.asarray(b_fox_f[l], f32), np.asarray(b_mlstm_i[l], f32), np.asarray(b_mlstm_f[l], f32)])
        small[:, l, 0:16] = gb[None, :]
        cw = np.asarray(conv_w[l], f32).reshape(4, 16, 128).transpose(2, 1, 0)
        small[:, l, 16:80] = cw.reshape(128, 64)
        small[:, l, 80:96] = np.asarray(conv_b[l], f32).reshape(16, 128).T
        small[:, l, 96:112] = np.asarray(mlstm_head_norm[l], f32).reshape(16, 128).T
    shared = {
        "gpre": gpre, "gpost": gpost, "small": small,
        "w_in": np.ascontiguousarray(np.asarray(w_in, f32)[:depth]),
        "w_a": np.ascontiguousarray(np.asarray(w_a, f32)[:depth]),
        "w_b": np.ascontiguousarray(np.asarray(w_b, f32)[:depth]),
        "w_o": np.ascontiguousarray(np.asarray(w_o, f32)[:depth]),
    }
    maps = []
    for b in range(B):
        xp = np.zeros((NTOK, D), f32)
        seq = np.concatenate([np.asarray(meta_tokens, f32), np.asarray(x[b], f32)], axis=0)
        n = min(NTOK, seq.shape[0])
        xp[:n] = seq[:n]
        m = dict(shared)
        m["x"] = xp
        maps.append(m)
    return maps


def run(inputs, ntiles, depth, trace=False):
    nc = build_program(ntiles, depth)
    maps = prep_inputs(ntiles=ntiles, depth=depth, **inputs)
    res = run_bass_kernel_spmd(nc, maps, core_ids=list(range(len(maps))), trace=trace)
    outs = [r["out"] for r in res.results]
    return np.stack(outs, axis=0), res


def kernel(**inputs):
    ntiles = (NMETA + SEQ + 127) // 128
    out, _ = run(inputs, ntiles, DEPTH)
    return np.ascontiguousarray(out[:, NMETA:NMETA + SEQ, :]).astype(np.float32)
```

```python
import os
import numpy as np
import ml_dtypes
from contextlib import ExitStack
import concourse.bass as bass
import concourse.mybir as mybir
from concourse.bass_utils import run_bass_kernel_spmd

F32 = mybir.dt.float32
BF16 = mybir.dt.bfloat16
AF = mybir.ActivationFunctionType
ALU = mybir.AluOpType

D = 1024
DEPTH = 4
NMETA = 16
SEQ = 4096
NIN = 14352
GT = 2
KSEG = 4
NKS = 6
KDIST = 4
NW = 4
C_FQ, C_FK, C_FV, C_FF, C_FZ = 0, 1024, 2048, 3072, 3080
C_MQK, C_MV, C_MI, C_MF, C_MO, C_MZ, C_GA, C_GB = 4104, 6152, 8200, 8204, 8208, 10256, 12304, 13328
EPS = 1e-6


class Op:
    __slots__ = ("eng", "fn", "deps", "dma_key", "ndma", "target", "flag", "is_dma")


class _FirstRec:
    def __init__(self, e):
        self._e = e
        self.first = None

    def __getattr__(self, name):
        attr = getattr(self._e, name)
        if not callable(attr):
            return attr

        def wrapped(*a, **k):
            r = attr(*a, **k)
            if self.first is None:
                self.first = r
            return r
        return wrapped


class Prog:
    ENGS = ("pe", "act", "dve", "pool", "sp")

    def __init__(self):
        self.ops = []
        self.last_w = {}
        self.readers = {}
        self.last_dma = {}
        self.dma_count = {}
        self.ops_index = {}

    def add(self, eng, fn, reads=(), writes=(), dma_key=None, ndma=1):
        op = Op()
        op.eng = eng
        op.fn = fn
        op.dma_key = dma_key
        op.is_dma = dma_key is not None
        op.ndma = ndma
        op.flag = 0
        op.target = 0
        deps = set()
        for r in reads:
            w = self.last_w.get(r)
            if w is not None:
                deps.add(w)
        for w_ in writes:
            lw = self.last_w.get(w_)
            if lw is not None:
                deps.add(lw)
            rd = self.readers.get(w_)
            if rd:
                deps.update(rd.values())
        for r in reads:
            d = self.readers.setdefault(r, {})
            d[("dma", dma_key) if op.is_dma else eng] = op
        for w_ in writes:
            self.last_w[w_] = op
            self.readers[w_] = {}
        if op.is_dma:
            prev = self.last_dma.get(dma_key)
            if prev is not None:
                deps.add(prev)
            self.last_dma[dma_key] = op
            n = self.dma_count.get(dma_key, 0) + ndma
            self.dma_count[dma_key] = n
            op.target = 16 * n
        deps.discard(op)
        op.deps = deps
        self.ops_index[op] = len(self.ops)
        self.ops.append(op)
        return op

    def emit(self, nc, stack, final_keys):
        idx = {}
        cnt_ = {}
        for op in self.ops:
            k = ("dma", op.dma_key) if op.is_dma else op.eng
            cnt_[k] = cnt_.get(k, 0) + 1
            idx[op] = (k, cnt_[k])
        K = {}
        last_on_eng = {}
        n_before = n_after = 0
        for op in self.ops:
            prev = last_on_eng.get(op.eng)
            cur = dict(K[prev]) if prev is not None else {}
            cand = []
            for d in op.deps:
                if (not d.is_dma) and d.eng == "pe" and op.eng == "pe" and not op.is_dma:
                    continue
                cand.append(d)
            n_before += len(cand)
            cand.sort(key=lambda d: -self.ops_index[d])
            keep = []
            for d in cand:
                k, i = idx[d]
                if cur.get(k, 0) >= i:
                    continue
                keep.append(d)
                for kk, vv in K[d].items():
                    if cur.get(kk, 0) < vv:
                        cur[kk] = vv
                if cur.get(k, 0) < i:
                    cur[k] = i
            n_after += len(keep)
            op.deps = set(keep)
            K[op] = cur
            last_on_eng[op.eng] = op
        print("[sync] dependency edges: %d -> %d after transitive reduction" % (n_before, n_after))
        need_flag = set()
        for op in self.ops:
            for d in op.deps:
                if d.is_dma:
                    continue
                if d.eng == "pe" and op.eng == "pe" and not op.is_dma:
                    continue
                need_flag.add(d)
        cnt = {e: 0 for e in self.ENGS}
        for op in self.ops:
            if (not op.is_dma) and op in need_flag:
                cnt[op.eng] += 1
                op.flag = cnt[op.eng]
        esem = {e: stack.enter_context(nc.semaphore("s_" + e)) for e in self.ENGS}
        dsem = {}
        for i, k in enumerate(self.dma_count):
            dsem[k] = stack.enter_context(nc.semaphore("d%d" % i))
        per_eng = {e: [] for e in self.ENGS}
        for op in self.ops:
            per_eng[op.eng].append(op)
        block = stack.enter_context(nc.Block())
        final = [(dsem[k], 16 * self.dma_count[k]) for k in final_keys]

        def run(eng_name, e):
            waited = {}
            for op in per_eng[eng_name]:
                waits = {}
                for d in op.deps:
                    if d.is_dma:
                        s, v = dsem[d.dma_key], d.target
                    else:
                        if d.eng == "pe" and eng_name == "pe" and not op.is_dma:
                            continue
                        s, v = esem[d.eng], d.flag
                    if waits.get(s, 0) < v:
                        waits[s] = v
                pending = [(s, v) for s, v in waits.items() if waited.get(s, 0) < v]
                for s, v in pending[:-1]:
                    e.wait_ge(s, v)
                    waited[s] = v
                rec = _FirstRec(e)
                r = op.fn(rec)
                if pending:
                    s, v = pending[-1]
                    assert rec.first is not None
                    rec.first._wait_ge(s, v)
                    waited[s] = v
                if op.is_dma:
                    if not isinstance(r, (list, tuple)):
                        r = [r]
                    assert len(r) == op.ndma
                    for ins in r:
                        ins.then_inc(dsem[op.dma_key], 16)
                elif op.flag:
                    r.then_inc(esem[eng_name], 1)
            if eng_name == "sp":
                for s, v in final:
                    e.wait_ge(s, v)

        @block.tensor
        def _(e):
            run("pe", e)

        @block.scalar
        def _(e):
            run("act", e)

        @block.vector
        def _(e):
            run("dve", e)

        @block.gpsimd
        def _(e):
            run("pool", e)

        @block.sync
        def _(e):
            run("sp", e)


def build_program(ntiles, depth):
    NTOK = ntiles * 128
    groups = [list(range(t, min(t + GT, ntiles))) for t in range(0, ntiles, GT)]
    NTM = GT * 128
    nc = bass.Bass("TRN2", target_bir_lowering=False)
    P = Prog()
    st = ExitStack()

    def dram(name, shape, dt, kind="Internal"):
        return nc.dram_tensor(name, shape, dt, kind=kind).ap()

    x_in = dram("x", [NTOK, D], F32, "ExternalInput")
    gpre_d = dram("gpre", [depth, 128, D], F32, "ExternalInput")
    gpost_d = dram("gpost", [depth, 128, D], F32, "ExternalInput")
    small_d = dram("small", [128, depth, 112], F32, "ExternalInput")
    w_in_d = dram("w_in", [depth, D, NIN], F32, "ExternalInput")
    w_a_d = dram("w_a", [depth, D, D], F32, "ExternalInput")
    w_b_d = dram("w_b", [depth, 2 * D, D], F32, "ExternalInput")
    w_o_d = dram("w_o", [depth, D, D], F32, "ExternalInput")
    out_d = dram("out", [NTOK, D], F32, "ExternalOutput")
    Hs = dram("Hs", [NTOK, D], F32)
    KTd = dram("KTd", [8, 128, NTOK], BF16)
    Vd = dram("Vd", [8, 128, ntiles, 128], BF16)
    Wb = dram("Wb", [depth, 39, 128, 8, 512], BF16)

    def sb(name, shape, dt):
        return st.enter_context(nc.sbuf_tensor("sb_" + name, shape, dt))

    def ps(name, shape, dt):
        return st.enter_context(nc.psum_tensor("ps_" + name, shape, dt))

    ident = sb("ident", [128, 128], BF16)
    mask01 = sb("mask01", [128, 128], BF16)
    maskneg = sb("maskneg", [128, 128], BF16)
    ones_bf = sb("ones_bf", [128, 128], BF16)
    tri_f = sb("tri_f", [128, 128], F32)
    ones_f = sb("ones_f", [128, 128], F32)
    cst = sb("cst", [128, 4], F32)
    mhalf = sb("mhalf", [128, 8], F32)
    small = sb("small", [128, depth, 112], F32)
    gpre = sb("gpre_t", [128, D], F32)
    gpost = sb("gpost_t", [128, D], F32)
    xt = [sb("xt%d" % i, [128, D], F32) for i in range(2)]
    junk = sb("junk", [128, D], BF16)
    ssA = sb("ssA", [128, 8], F32)
    xb = [sb("xb%d" % i, [128, D], BF16) for i in range(2)]
    hnTb = [sb("hnT%d" % i, [128, 8, NTM], BF16) for i in range(2)]
    Wsl = [sb("W%d" % i, [128, 8, 512], BF16) for i in range(NW)]
    vnew = sb("vnew", [128, GT, 8, 128], BF16)
    qT = [sb("qT%d" % i, [128, NTM], BF16) for i in range(2)]
    kTn = [sb("kTn%d" % i, [128, NTM], BF16) for i in range(2)]
    sfz = [sb("sfz%d" % i, [128, NTM], BF16) for i in range(2)]
    KTs = [sb("KTs%d" % i, [128, KSEG * 128], BF16) for i in range(NKS)]
    Vs = [sb("Vs%d" % i, [128, KSEG, 128], BF16) for i in range(NKS)]
    PT = [sb("PT%d" % i, [128, NTM], BF16) for i in range(4)]
    rl = [sb("rl%d" % i, [128, NTM], F32) for i in range(2)]
    Osb = [sb("Osb%d" % i, [128, NTM], F32) for i in range(2)]
    xaT = sb("xaT", [128, 8, NTM], BF16)
    Fcum = sb("Fcum", [128, ntiles, 8], F32)
    carry = sb("carry", [128, 8], F32)
    Fref = sb("Fref", [128, 8], F32)
    biasH = [sb("biasH%d" % i, [128, ntiles], F32) for i in range(2)]
    gz = sb("gz", [128, 16], F32)
    ge = sb("ge", [128, 16], F32)
    gsp = sb("gsp", [128, 16], F32)
    lg = sb("lg", [128, 12], F32)
    bloc = sb("bloc", [128, GT, 4], F32)
    blast = sb("blast", [128, GT, 4], F32)
    gtmp = sb("gtmp", [128, GT, 8], F32)
    gsT = sb("gsT", [128, GT, 4], F32)
    gdsT = sb("gdsT", [128, GT, 4], F32)
    ehT = sb("ehT", [128, GT, 4], F32)
    decT = sb("decT", [128, GT, 4], F32)
    stage = [sb("stage%d" % i, [128, NTM + 3], BF16) for i in range(3)]
    hist = sb("hist", [128, 16, 3], BF16)
    cdiag = sb("cdiag", [128, 64, 128], BF16)
    qkTb = [sb("qkT%d" % i, [128, 4, NTM], BF16) for i in range(2)]
    smz = [sb("smz%d" % i, [128, NTM], BF16) for i in range(2)]
    smzgb = [sb("smzg%d" % i, [128, 4, NTM], BF16) for i in range(2)]
    vmb = [sb("vm%d" % i, [128, GT, 512], BF16) for i in range(2)]
    sgob = [sb("sgo%d" % i, [128, GT, 512], BF16) for i in range(2)]
    ktok = [sb("ktok%d" % i, [128, 256], BF16) for i in range(2)]
    STm = [sb("STm%d" % i, [128, 128], BF16) for i in range(2)]
    dtmp = sb("dtmp", [128, 8], F32)
    hb = sb("hb", [128, GT, 512], F32)
    ssq = sb("ssq", [128, 8], F32)
    hbn = [sb("hbn%d" % i, [128, 512], BF16) for i in range(2)]
    hbT = sb("hbT", [128, 16, NTM], BF16)
    Cf = sb("Cf", [128, 8, 512], F32)
    Cb = sb("Cb", [128, 8, 512], BF16)
    nf = sb("nf", [128, 8], F32)
    nb = sb("nb", [128, 8], BF16)
    sga = sb("sga", [128, 8, NTM], BF16)
    sgb = sb("sgb", [128, 8, NTM], BF16)
    etmp = [sb("etmp%d" % i, [128, NTM], F32) for i in range(2)]
    etmpA = [sb("etmpA%d" % i, [128, NTM], F32) for i in range(4)]
    mrg = sb("mrg", [128, 8, NTM], BF16)
    otile = [sb("otile%d" % i, [128, D], F32) for i in range(2)]
    xres = [sb("xres%d" % i, [128, D], F32) for i in range(2)]
    ssE = sb("ssE", [128, 8], F32)
    pa = [ps("pa%d" % i, [128, 512], F32) for i in range(4)]
    pO = ps("pO", [128, 512], F32)
    pL = ps("pL", [128, 512], F32)
    ptr = ps("ptr", [128, 8, 128], BF16)
    psm = ps("psm", [128, 512], F32)

    rot = {}

    def nxt(key, n):
        v = rot.get(key, 0)
        rot[key] = v + 1
        return v % n

    def c_memset(t, val):
        P.add("pool", lambda e, t=t, val=val: e.memset(t[:], val), writes=[("c", id(t))])

    def c_select(t, fill, keep_ge):
        if keep_ge:
            P.add("pool", lambda e, t=t: e.affine_select(out=t[:], in_=t[:], pattern=[[1, 128]],
                                                        compare_op=ALU.is_ge, fill=fill, base=0,
                                                        channel_multiplier=-1),
                  reads=[("c", id(t))], writes=[("c", id(t))])
        else:
            P.add("pool", lambda e, t=t: e.affine_select(out=t[:], in_=t[:], pattern=[[-1, 128]],
                                                        compare_op=ALU.not_equal, fill=fill, base=0,
                                                        channel_multiplier=1),
                  reads=[("c", id(t))], writes=[("c", id(t))])

    c_memset(ident, 0.0)
    c_select(ident, 1.0, False)
    c_memset(mask01, 1.0)
    c_select(mask01, 0.0, True)
    c_memset(maskneg, 0.0)
    c_select(maskneg, -30000.0, True)
    c_memset(ones_bf, 1.0)
    c_memset(tri_f, 1.0)
    c_select(tri_f, 0.0, True)
    c_memset(ones_f, 1.0)
    P.add("pool", lambda e: e.memset(mhalf[:], -0.5), writes=[("c", "mhalf")])
    P.add("pool", lambda e: e.memset(cst[:, 0:1], EPS), writes=[("c", "cst0")])
    P.add("pool", lambda e: e.memset(cst[:, 1:2], 1.0), writes=[("c", "cst1")])
    P.add("pool", lambda e: e.memset(cst[:, 2:3], -float(np.log(16.0))), writes=[("c", "cst2")])
    CONST = [("c", id(ident)), ("c", id(mask01)), ("c", id(maskneg)), ("c", id(ones_bf)),
             ("c", id(tri_f)), ("c", id(ones_f)), ("c", "cst0"), ("c", "cst1"), ("c", "cst2"), ("c", "mhalf")]
    P.add("sp", lambda e: e.dma_start(out=small[:], in_=small_d[:, :, :]), writes=["small"],
          dma_key="small")

    wstate = {"units": [], "issued": 0, "cur_layer": 0, "cur_uidx": 0}
    NUNITS = 39

    def w_unit(parts):
        wstate["units"].append((wstate["cur_layer"], wstate["cur_uidx"], parts))
        wstate["cur_uidx"] += 1
        return len(wstate["units"]) - 1

    def w_issue_upto(u):
        while wstate["issued"] <= u and wstate["issued"] < len(wstate["units"]):
            i = wstate["issued"]
            slot = i % NW
            l_, uidx, parts = wstate["units"][i]
            if l_ == 0:
                def fn(e, slot=slot, parts=parts):
                    r = []
                    for (c0, ncols, src) in parts:
                        r.append(e.dma_start(out=Wsl[slot][:, :, c0:c0 + ncols],
                                             in_=src.rearrange("(k p) n -> p k n", p=128)))
                    return r
                P.add("pool", fn, writes=[("W", slot)], dma_key=("W", slot), ndma=len(parts))
            else:
                used = max(c0 + ncols for (c0, ncols, _) in parts)

                def fn(e, slot=slot, l_=l_, uidx=uidx, used=used):
                    return [e.dma_start(out=Wsl[slot][:, :, 0:used], in_=Wb[l_, uidx, :, :, 0:used])]
                P.add("pool", fn, reads=[("Wb", l_)], writes=[("W", slot)], dma_key=("W", slot), ndma=1)
            wstate["issued"] += 1

    def conv_chunks(l_, nchunks):
        first = next(i for i, (ll, ui, _) in enumerate(wstate["units"]) if ll == l_ and ui == 0)
        dmas = []
        for (ll, uidx, parts) in wstate["units"][first:first + NUNITS]:
            assert ll == l_
            for (c0, ncols, src) in parts:
                dmas.append((uidx, c0, ncols, src))
        per = (len(dmas) + nchunks - 1) // nchunks
        chunks = []
        for ci in range(nchunks):
            sub = dmas[ci * per:(ci + 1) * per]
            if not sub:
                chunks.append(None)
                continue

            def emit(sub=sub):
                def fn(e):
                    return [e.dma_start(out=Wb[l_, uidx, :, :, c0:c0 + ncols],
                                        in_=src.rearrange("(k p) n -> p k n", p=128))
                            for (uidx, c0, ncols, src) in sub]
                P.add("pool", fn, writes=[("Wb", l_)], dma_key=("wconv", l_), ndma=len(sub))
            chunks.append(emit)
        return chunks

    def w_get(u, first_live=None):
        fl = u if first_live is None else first_live
        w_issue_upto(max(u, fl + NW - 1))
        slot = u % NW
        return Wsl[slot], ("W", slot)

    unit_plan = []

    def plan_layer(l):
        pl = {}
        wi = w_in_d
        per_group = []
        for g in range(len(groups)):
            u = {}
            wstate["cur_layer"] = l
            wstate["cur_uidx"] = 0
            u["fv"] = [w_unit([(0, 512, wi[l, :, C_FV + j * 512:C_FV + (j + 1) * 512])]) for j in range(2)]
            u["gate"] = w_unit([(0, 8, wi[l, :, C_FF:C_FF + 8]), (8, 8, wi[l, :, C_MI:C_MI + 8])])
            u["fox"] = []
            for h in range(8):
                u["fox"].append(w_unit([(0, 128, wi[l, :, C_FQ + h * 128:C_FQ + (h + 1) * 128]),
                                        (128, 128, wi[l, :, C_FK + h * 128:C_FK + (h + 1) * 128]),
                                        (256, 128, wi[l, :, C_FZ + h * 128:C_FZ + (h + 1) * 128])]))
            u["ml"] = []
            for h in range(4):
                ua = w_unit([(0, 256, wi[l, :, C_MQK + h * 256:C_MQK + (h + 1) * 256]),
                             (256, 256, wi[l, :, C_MQK + 1024 + h * 256:C_MQK + 1024 + (h + 1) * 256])])
                ub = w_unit([(0, 512, wi[l, :, C_MZ + h * 512:C_MZ + (h + 1) * 512])])
                uc = w_unit([(0, 512, wi[l, :, C_MV + h * 512:C_MV + (h + 1) * 512])])
                ud = w_unit([(0, 512, wi[l, :, C_MO + h * 512:C_MO + (h + 1) * 512])])
                u["ml"].append((ua, ub, uc, ud))
            u["half"] = []
            for hf in range(2):
                uga = w_unit([(0, 512, wi[l, :, C_GA + hf * 512:C_GA + (hf + 1) * 512])])
                ugb = w_unit([(0, 512, wi[l, :, C_GB + hf * 512:C_GB + (hf + 1) * 512])])
                uwa = w_unit([(0, 512, w_a_d[l, :, hf * 512:(hf + 1) * 512])])
                uwb0 = w_unit([(0, 512, w_b_d[l, 0:1024, hf * 512:(hf + 1) * 512])])
                uwb1 = w_unit([(0, 512, w_b_d[l, 1024:2048, hf * 512:(hf + 1) * 512])])
                u["half"].append((uga, ugb, uwa, uwb0, uwb1))
            u["wo"] = [w_unit([(0, 512, w_o_d[l, :, j * 512:(j + 1) * 512])]) for j in range(2)]
            assert wstate["cur_uidx"] == NUNITS
            per_group.append(u)
        pl["groups"] = per_group
        return pl

    for l in range(depth):
        unit_plan.append(plan_layer(l))

    def proj_feat(W, wkey, c0, rhs_t, rhs_key, nk, NT, dst_ps, dst_key, koff=0, extra_reads=()):
        def fn(e):
            r = None
            for k in range(nk):
                r = e.matmul(dst_ps[:, 0:NT], lhsT=W[:, k, c0:c0 + 128], rhs=rhs_t[:, koff + k, 0:NT],
                             start=(k == 0), stop=(k == nk - 1))
            return r
        P.add("pe", fn, reads=[wkey, rhs_key] + list(extra_reads), writes=[dst_key])

    def proj_tok(W, wkey, ncols, ti, dst_ps, dst_key, hnT, HK):
        def fn(e):
            r = None
            for k in range(8):
                r = e.matmul(dst_ps[:, 0:ncols], lhsT=hnT[:, k, ti * 128:(ti + 1) * 128],
                             rhs=W[:, k, 0:ncols], start=(k == 0), stop=(k == 7))
            return r
        P.add("pe", fn, reads=[wkey, HK], writes=[dst_key])

    def pa_next():
        i = nxt("pa", 4)
        return pa[i], ("pa", i)

    def layer(l, H_in, hin_name, H_out, hout_name, par0):
        plan = unit_plan[l]
        sm = small[:, l, :]

        def pre_A():
            P.add("sp", lambda e: e.dma_start(out=gpre[:], in_=gpre_d[l, :, :]), writes=["gpre"], dma_key="gpre")

        def pre_body():
            P.add("sp", lambda e: e.dma_start(out=gpost[:], in_=gpost_d[l, :, :]), writes=["gpost"], dma_key="gpost")
            P.add("pool", lambda e: e.memset(Cf[:], 0.0), writes=[("Cf", i) for i in range(8)])
            P.add("pool", lambda e: e.memset(Cb[:], 0.0), writes=[("Cb", i) for i in range(8)])
            P.add("pool", lambda e: e.memset(nf[:], 0.0), writes=["nf"])
            P.add("pool", lambda e: e.memset(nb[:], 0.0), writes=["nb"])
            P.add("pool", lambda e: e.memset(hist[:], 0.0), writes=[("hist", i) for i in range(16)])
            P.add("pool", lambda e: e.memset(carry[:], 0.0), writes=["carry"])
            for j in range(64):
                P.add("dve", lambda e, j=j: e.tensor_scalar(out=cdiag[:, j, :], in0=ident[:], scalar1=sm[:, 16 + j:17 + j],
                                                          scalar2=None, op0=ALU.mult),
                      reads=["small"] + CONST, writes=["cdiag"])

        def make_group(g, tiles, par):
            U = plan["groups"][g]
            nt = len(tiles)
            NT = nt * 128
            T0 = tiles[0]
            tok0 = T0 * 128

            hnT = hnTb[par]
            HK = ("hnT", par)

            def gate_items(hf):
                uga, ugb = U["half"][hf][0], U["half"][hf][1]
                items = []

                def one(u, dst, dname, mm):
                    WG, wgk = w_get(u)
                    m = hf * 4 + mm
                    pc, pck = pa_next()
                    proj_feat(WG, wgk, mm * 128, hnT, HK, 8, NT, pc, pck)
                    P.add("act", lambda e: e.activation(out=dst[:, m, 0:NT], in_=pc[:, 0:NT], func=AF.Sigmoid),
                          reads=[pck], writes=[(dname, m)])
                for mm in range(4):
                    items.append(lambda mm=mm: one(uga, sga, "sga", mm))
                for mm in range(4):
                    items.append(lambda mm=mm: one(ugb, sgb, "sgb", mm))
                return items

            def stageA():
                for i, T in enumerate(tiles):
                    s = nxt("xt", 2)
                    P.add("sp", lambda e, s=s, T=T: e.dma_start(out=xt[s][:], in_=H_in[T * 128:(T + 1) * 128, :]),
                          reads=[(hin_name, T)], writes=[("xt", s)], dma_key=("xt", s))
                    P.add("act", lambda e, s=s, i=i: e.activation(out=junk[:], in_=xt[s][:], func=AF.Square,
                                                                accum_out=ssA[:, i:i + 1]),
                          reads=[("xt", s)], writes=["junk", ("ssA", i)])
                P.add("pool", lambda e: e.tensor_scalar(out=ssA[:, 4:4 + nt], in0=ssA[:, 0:0 + nt],
                                                        scalar1=1.0 / D, scalar2=EPS, op0=ALU.mult, op1=ALU.add),
                      reads=[("ssA", i) for i in range(nt)], writes=["ssA_s"])
                P.add("pool", lambda e: e.tensor_tensor(out=ssA[:, 4:4 + nt], in0=ssA[:, 4:4 + nt],
                                                        in1=mhalf[:, 0:nt], op=ALU.pow),
                      reads=["ssA_s"] + CONST, writes=["ssA_s"])
                base_slot = (rot["xt"] - nt) % 2
                for i, T in enumerate(tiles):
                    s = (base_slot + i) % 2
                    b = nxt("xb", 2)
                    P.add("dve", lambda e, s=s, b=b, i=i: e.scalar_tensor_tensor(
                        out=xb[b][:], in0=xt[s][:], scalar=ssA[:, 4 + i:5 + i], in1=gpre[:],
                        op0=ALU.mult, op1=ALU.mult),
                        reads=[("xt", s), "ssA_s", "gpre"], writes=[("xb", b)])

                    def fn(e, b=b):
                        r = None
                        for k in range(8):
                            r = e.transpose(out=ptr[:, k, :], in_=xb[b][:, k * 128:(k + 1) * 128], identity=ident[:])
                        return r
                    P.add("pe", fn, reads=[("xb", b)] + CONST, writes=["ptr"])
                    P.add("dve", lambda e, i=i: e.tensor_copy(out=hnT[:, :, i * 128:(i + 1) * 128], in_=ptr[:, :, :]),
                          reads=["ptr"], writes=[HK])


            def body():
                for j in range(2):
                    W, wk = w_get(U["fv"][j])
                    for i, T in enumerate(tiles):
                        pt_, pk = pa_next()
                        proj_tok(W, wk, 512, i, pt_, pk, hnT, HK)
                        P.add("act", lambda e, i=i, j=j, pt_=pt_: e.copy(
                            out=vnew[:, i, j * 4:(j + 1) * 4, :], in_=pt_[:, :].rearrange("p (h d) -> p h d", h=4)),
                            reads=[pk], writes=[("vnew", i)])
                for i, T in enumerate(tiles):
                    P.add("sp", lambda e, i=i, T=T: e.dma_start(out=Vd[:, :, T, :].rearrange("h p d -> p h d"),
                                                              in_=vnew[:, i, :, :]),
                          reads=[("vnew", i)], writes=["Vd"], dma_key=("vst", i))
                W, wk = w_get(U["gate"])
                for i, T in enumerate(tiles):
                    proj_tok(W, wk, 16, i, psm, "psm", hnT, HK)
                    P.add("dve", lambda e: e.tensor_tensor(out=gz[:], in0=psm[:, 0:16], in1=sm[:, 0:16], op=ALU.add),
                          reads=["psm", "small"], writes=["gz"])
                    P.add("act", lambda e: e.activation(out=ge[:], in_=gz[:], func=AF.Exp, scale=-1.0),
                          reads=["gz"], writes=["ge"])
                    P.add("act", lambda e: e.activation(out=gsp[:], in_=ge[:], func=AF.Ln, bias=cst[:, 1:2]),
                          reads=["ge"] + CONST, writes=["gsp"])
                    P.add("dve", lambda e: e.tensor_scalar(out=lg[:, 0:8], in0=gsp[:, 0:8], scalar1=-1.0, scalar2=None,
                                                           op0=ALU.mult), reads=["gsp"], writes=["lg0"])
                    P.add("dve", lambda e: e.tensor_scalar(out=lg[:, 8:12], in0=gsp[:, 12:16], scalar1=-1.0,
                                                           scalar2=None, op0=ALU.mult), reads=["gsp"], writes=["lg1"])
                    P.add("pe", lambda e: e.matmul(psm[:, 32:44], lhsT=tri_f[:], rhs=lg[:], start=True, stop=True),
                          reads=["lg0", "lg1"] + CONST, writes=["psm"])
                    P.add("dve", lambda e, T=T: e.tensor_tensor(out=Fcum[:, T, :], in0=psm[:, 32:40], in1=carry[:],
                                                               op=ALU.add),
                          reads=["psm", "carry"], writes=["Fcum"])
                    P.add("dve", lambda e, i=i: e.tensor_copy(out=bloc[:, i, :], in_=psm[:, 40:44]),
                          reads=["psm"], writes=[("bloc", i)])
                    P.add("pe", lambda e: e.matmul(psm[:, 64:76], lhsT=ones_f[:], rhs=lg[:], start=True, stop=True),
                          reads=["lg0", "lg1", "Fcum", ("bloc", i)] + CONST, writes=["psm"])
                    P.add("dve", lambda e: e.tensor_tensor(out=carry[:], in0=psm[:, 64:72], in1=carry[:], op=ALU.add),
                          reads=["psm", "carry", "Fcum"], writes=["carry"])
                    P.add("dve", lambda e, i=i: e.tensor_copy(out=blast[:, i, :], in_=psm[:, 72:76]),
                          reads=["psm"], writes=[("blast", i)])
                    if i == 0:
                        P.add("dve", lambda e: e.tensor_copy(out=Fref[:], in_=carry[:]), reads=["carry"], writes=["Fref"])
                    P.add("dve", lambda e, i=i: e.tensor_tensor(out=gtmp[:, i, 0:4], in0=gz[:, 8:12], in1=bloc[:, i, :],
                                                               op=ALU.subtract),
                          reads=["gz", ("bloc", i)], writes=[("gtmp0", i)])
                    P.add("dve", lambda e, i=i: e.tensor_tensor(out=gtmp[:, i, 4:8], in0=gtmp[:, i, 0:4],
                                                               in1=blast[:, i, :], op=ALU.add),
                          reads=[("gtmp0", i), ("blast", i)], writes=[("gtmp1", i)])
                    P.add("act", lambda e, i=i: e.activation(out=gsT[:, i, :], in_=gtmp[:, i, 0:4], func=AF.Exp),
                          reads=[("gtmp0", i)], writes=[("gsT", i)])
                    P.add("act", lambda e, i=i: e.activation(out=gdsT[:, i, :], in_=gtmp[:, i, 4:8], func=AF.Exp),
                          reads=[("gtmp1", i)], writes=[("gdsT", i)])
                    P.add("act", lambda e, i=i: e.activation(out=ehT[:, i, :], in_=bloc[:, i, :], func=AF.Exp,
                                                             bias=cst[:, 2:3]),
                          reads=[("bloc", i)] + CONST, writes=[("ehT", i)])
                    P.add("act", lambda e, i=i: e.activation(out=decT[:, i, :], in_=blast[:, i, :], func=AF.Exp),
                          reads=[("blast", i)], writes=[("decT", i)])


                nkb = tiles[-1] + 1

                def fox_proj(h):
                    W, wk = w_get(U["fox"][h])
                    hb_ = h % 2
                    pq, pqk = pa_next()
                    proj_feat(W, wk, 0, hnT, HK, 8, NT, pq, pqk)
                    P.add("dve", lambda e, hb_=hb_, pq=pq: e.tensor_scalar(
                        out=qT[hb_][:, 0:NT], in0=pq[:, 0:NT], scalar1=float(128 ** -0.5), scalar2=None, op0=ALU.mult),
                        reads=[pqk], writes=[("qT", hb_)])
                    pk_, pkk = pa_next()
                    proj_feat(W, wk, 128, hnT, HK, 8, NT, pk_, pkk)
                    P.add("dve", lambda e, hb_=hb_, pk_=pk_: e.tensor_copy(out=kTn[hb_][:, 0:NT], in_=pk_[:, 0:NT]),
                          reads=[pkk], writes=[("kTn", hb_)])
                    P.add("sp", lambda e, hb_=hb_, h=h: e.dma_start(
                        out=KTd[h, :, tok0:tok0 + NT], in_=kTn[hb_][:, 0:NT]),
                        reads=[("kTn", hb_)], writes=[("KTd", h)], dma_key=("kst", hb_))
                    pz, pzk = pa_next()
                    proj_feat(W, wk, 256, hnT, HK, 8, NT, pz, pzk)
                    P.add("act", lambda e, hb_=hb_, pz=pz: e.activation(out=sfz[hb_][:, 0:NT], in_=pz[:, 0:NT],
                                                                       func=AF.Tanh, scale=0.5),
                          reads=[pzk], writes=[("sfz", hb_)])
                    P.add("dve", lambda e, hb_=hb_, pz=pz: e.scalar_tensor_tensor(
                        out=sfz[hb_][:, 0:NT], in0=sfz[hb_][:, 0:NT], scalar=1.0, in1=pz[:, 0:NT],
                        op0=ALU.add, op1=ALU.mult),
                        reads=[pzk, ("sfz", hb_)], writes=[("sfz", hb_)])
                    P.add("dve", lambda e, hb_=hb_, h=h: e.tensor_scalar(
                        out=biasH[hb_][:, 0:nkb], in0=Fcum[:, 0:nkb, h], scalar1=-1.0, scalar2=Fref[:, h:h + 1],
                        op0=ALU.mult, op1=ALU.add),
                        reads=["Fcum", "Fref"], writes=[("biasH", hb_)])

                seg_slot = {}

                def fox_load(h, si):
                    s0 = si * KSEG
                    nb_ = min(KSEG, nkb - s0)
                    ks = nxt("kseg", NKS)
                    seg_slot[(h, si)] = ks
                    P.add("sp", lambda e: e.dma_start(
                        out=KTs[ks][:, 0:nb_ * 128], in_=KTd[h, :, s0 * 128:(s0 + nb_) * 128]),
                        reads=[("KTd", h)], writes=[("KTs", ks)], dma_key=("KTs", ks))
                    P.add("sp", lambda e: e.dma_start(
                        out=Vs[ks][:, 0:nb_, :], in_=Vd[h, :, s0:s0 + nb_, :]),
                        reads=["Vd"], writes=[("Vs", ks)], dma_key=("Vs", ks))

                nseg = (nkb + KSEG - 1) // KSEG
                load_list = [(hh, si) for hh in range(8) for si in range(nseg)]
                load_state = {"n": 0}

                def ensure_loads(h, si, maxh=None):
                    upto = load_list.index((h, si)) + KDIST
                    while load_state["n"] < len(load_list) and load_state["n"] <= upto:
                        hh, s2 = load_list[load_state["n"]]
                        if hh > (h + 1 if maxh is None else maxh):
                            break
                        fox_load(hh, s2)
                        load_state["n"] += 1

                def fox_attn(h):
                    hb_ = h % 2
                    blocks = []
                    for si in range(nseg):
                        s0 = si * KSEG
                        nb_ = min(KSEG, nkb - s0)
                        for jj in range(nb_):
                            blocks.append((si, jj, s0 + jj))
                    st_ = {}

                    def qk_exp(n):
                        si, jj, j = blocks[n]
                        if jj == 0:
                            ensure_loads(h, si)
                        ks = seg_slot[(h, si)]
                        diag = j >= T0
                        q0 = (j - T0) * 128 if diag else 0
                        pS, pSk = pa_next()
                        p_ = nxt("PT", 4)
                        st_[n] = (p_, q0)

                        def fn(e):
                            r = e.matmul(pS[:, q0:NT], lhsT=KTs[ks][:, jj * 128:(jj + 1) * 128],
                                         rhs=qT[hb_][:, q0:NT], start=True, stop=(not diag))
                            if diag:
                                r = e.matmul(pS[:, q0:q0 + 128], lhsT=ident[:], rhs=maskneg[:],
                                             start=False, stop=True)
                            return r
                        P.add("pe", fn, reads=[("KTs", ks), ("qT", hb_)] + CONST, writes=[pSk])
                        P.add("act", lambda e: e.activation(
                            out=PT[p_][:, q0:NT], in_=pS[:, q0:NT], func=AF.Exp, bias=biasH[hb_][:, j:j + 1]),
                            reads=[pSk, ("biasH", hb_)], writes=[("PT", p_)])

                    def pv(n):
                        si, jj, j = blocks[n]
                        ks = seg_slot[(h, si)]
                        p_, q0 = st_[n]

                        def fn2(e):
                            e.matmul(pO[:, q0:NT], lhsT=Vs[ks][:, jj, :], rhs=PT[p_][:, q0:NT],
                                     start=(j == 0), stop=(j == nkb - 1))
                            return e.matmul(pL[:, q0:NT], lhsT=ones_bf[:], rhs=PT[p_][:, q0:NT],
                                            start=(j == 0), stop=(j == nkb - 1))
                        P.add("pe", fn2, reads=[("Vs", ks), ("PT", p_)] + CONST, writes=["pO", "pL"])

                    nblk = len(blocks)
                    AHEAD = 3
                    for n in range(min(AHEAD, nblk)):
                        qk_exp(n)
                    for n in range(nblk):
                        pv(n)
                        if n + AHEAD < nblk:
                            qk_exp(n + AHEAD)
                    r_ = nxt("rl", 2)
                    P.add("dve", lambda e: e.tensor_copy(out=rl[r_][:, 0:NT], in_=pL[:, 0:NT]),
                          reads=["pL"], writes=[("rl", r_)])
                    P.add("dve", lambda e: e.tensor_copy(out=Osb[r_][:, 0:NT], in_=pO[:, 0:NT]),
                          reads=["pO"], writes=[("Osb", r_)])
                    P.add("dve", lambda e: e.reciprocal(out=rl[r_][:, 0:NT], in_=rl[r_][:, 0:NT]),
                          reads=[("rl", r_)], writes=[("rl", r_)])
                    P.add("dve", lambda e: e.scalar_tensor_tensor(
                        out=rl[r_][:, 0:NT], in0=rl[r_][:, 0:NT], scalar=0.5, in1=sfz[hb_][:, 0:NT],
                        op0=ALU.mult, op1=ALU.mult),
                        reads=[("rl", r_), ("sfz", hb_)], writes=[("rl", r_)])
                    P.add("dve", lambda e: e.tensor_tensor(
                        out=xaT[:, h, 0:NT], in0=Osb[r_][:, 0:NT], in1=rl[r_][:, 0:NT], op=ALU.mult),
                        reads=[("Osb", r_), ("rl", r_)], writes=["xaT"])


                def ml_items(h):
                    s_ = h % 2
                    ua, ub, uc, ud = U["ml"][h]
                    qk_, smzg_, vm_, sgo_ = qkTb[s_], smzgb[s_], vmb[s_], sgob[s_]

                    cst_ = {}

                    def conv_p(c):
                        WA, wak = w_get(ua)
                        ch = (2 * h + c) if c < 2 else (8 + 2 * h + (c - 2))
                        pc, pck = pa_next()
                        proj_feat(WA, wak, c * 128, hnT, HK, 8, NT, pc, pck)
                        sg = nxt("stage", 3)
                        cst_[c] = (sg, ch)
                        P.add("act", lambda e: e.copy(out=stage[sg][:, 3:3 + NT], in_=pc[:, 0:NT]),
                              reads=[pck], writes=[("stage", sg)])
                        P.add("dve", lambda e: e.tensor_copy(out=stage[sg][:, 0:3], in_=hist[:, ch, :]),
                              reads=[("hist", ch)], writes=[("stageh", sg)])

                    def conv_c(c):
                        sg, ch = cst_[c]
                        pv_, pvk = pa_next()

                        def fn(e):
                            r = None
                            for tp in range(4):
                                r = e.matmul(pv_[:, 0:NT], lhsT=cdiag[:, ch * 4 + tp, :], rhs=stage[sg][:, tp:tp + NT],
                                             start=(tp == 0), stop=(tp == 3))
                            return r
                        P.add("pe", fn, reads=[("stage", sg), ("stageh", sg), "cdiag"], writes=[pvk])
                        P.add("dve", lambda e: e.tensor_copy(out=hist[:, ch, :], in_=stage[sg][:, NT:NT + 3]),
                              reads=[("stage", sg), ("stageh", sg)], writes=[("hist", ch)])
                        P.add("act", lambda e: e.activation(out=qk_[:, c, 0:NT], in_=pv_[:, 0:NT], func=AF.Silu,
                                                            bias=sm[:, 80 + ch:81 + ch]),
                              reads=[pvk, "small"], writes=[("qkT", s_, c)])

                    def mz_chunk(c):
                        WB, wbk = w_get(ub)
                        pc, pck = pa_next()
                        proj_feat(WB, wbk, c * 128, hnT, HK, 8, NT, pc, pck)
                        z_ = nxt("smz", 2)
                        P.add("act", lambda e: e.activation(out=smz[z_][:, 0:NT], in_=pc[:, 0:NT], func=AF.Silu),
                              reads=[pck], writes=[("smz", z_)])
                        P.add("dve", lambda e: e.tensor_scalar(
                            out=smzg_[:, c, 0:NT], in0=smz[z_][:, 0:NT], scalar1=sm[:, 96 + h * 4 + c:97 + h * 4 + c],
                            scalar2=None, op0=ALU.mult),
                            reads=[("smz", z_), "small"], writes=[("smzg", s_, c)])

                    def v_tile(i):
                        WC, wck = w_get(uc)
                        pc, pck = pa_next()
                        proj_tok(WC, wck, 512, i, pc, pck, hnT, HK)
                        P.add("act", lambda e: e.copy(out=vm_[:, i, :], in_=pc[:, :]), reads=[pck], writes=[("vm", s_, i)])

                    def o_tile(i):
                        WD, wdk = w_get(ud)
                        pc, pck = pa_next()
                        proj_tok(WD, wdk, 512, i, pc, pck, hnT, HK)
                        P.add("act", lambda e: e.activation(out=sgo_[:, i, :], in_=pc[:, :], func=AF.Sigmoid),
                              reads=[pck], writes=[("sgo", s_, i)])

                    items = []
                    for (kind, c) in (("p", 0), ("p", 1), ("c", 0), ("p", 2), ("c", 1), ("p", 3), ("c", 2), ("c", 3)):
                        items.append((lambda c=c: conv_p(c)) if kind == "p" else (lambda c=c: conv_c(c)))
                    for c in range(4):
                        items.append(lambda c=c: mz_chunk(c))
                    for i in range(nt):
                        items.append(lambda i=i: v_tile(i))
                    for i in range(nt):
                        items.append(lambda i=i: o_tile(i))
                    return items

                def ml_head(h, filler):
                    s_ = h % 2
                    qk_, smzg_, vm_, sgo_ = qkTb[s_], smzgb[s_], vmb[s_], sgob[s_]
                    QK = [("qkT", s_, c) for c in range(4)]

                    per_tile = (len(filler) * 3) // (4 * nt) if nt else 0

                    def fill(n):
                        while n > 0 and filler:
                            filler.pop(0)()
                            n -= 1

                    for i in range(nt):
                        cs = slice(i * 128, (i + 1) * 128)
                        kt = nxt("ktok", 2)
                        sT = nxt("STm", 2)

                        def fn(e, cs=cs):
                            e.transpose(out=ptr[:, 0, :], in_=qk_[:, 2, cs], identity=ident[:])
                            return e.transpose(out=ptr[:, 1, :], in_=qk_[:, 3, cs], identity=ident[:])
                        P.add("pe", fn, reads=[QK[2], QK[3]] + CONST, writes=["ptr"])
                        P.add("dve", lambda e, kt=kt, i=i: e.tensor_scalar(
                            out=ktok[kt][:, :], in0=ptr[:, 0:2, :].rearrange("p a b -> p (a b)"),
                            scalar1=gdsT[:, i, h:h + 1], scalar2=None, op0=ALU.mult),
                            reads=["ptr", ("gdsT", i)], writes=[("ktok", kt)])

                        def fn(e, cs=cs):
                            e.matmul(pO[:, 0:128], lhsT=qk_[:, 2, cs], rhs=qk_[:, 0, cs], start=True, stop=False)
                            return e.matmul(pO[:, 0:128], lhsT=qk_[:, 3, cs], rhs=qk_[:, 1, cs], start=False, stop=True)
                        P.add("pe", fn, reads=QK, writes=["pO"])
                        P.add("dve", lambda e, sT=sT, i=i: e.scalar_tensor_tensor(
                            out=STm[sT][:], in0=pO[:, 0:128], scalar=gsT[:, i, h:h + 1], in1=mask01[:],
                            op0=ALU.mult, op1=ALU.mult),
                            reads=["pO", ("gsT", i)] + CONST, writes=[("STm", sT)])
                        pn, pnk = pa_next()

                        def fn(e, sT=sT, i=i, cs=cs, pn=pn):
                            e.matmul(pn[:, :], lhsT=STm[sT][:], rhs=vm_[:, i, :], start=True, stop=False)
                            e.matmul(pn[:, :], lhsT=qk_[:, 0, cs], rhs=Cb[:, 2 * h, :], start=False, stop=False)
                            return e.matmul(pn[:, :], lhsT=qk_[:, 1, cs], rhs=Cb[:, 2 * h + 1, :], start=False, stop=True)
                        P.add("pe", fn, reads=[("STm", sT), ("vm", s_, i), QK[0], QK[1], ("Cb", 2 * h),
                                               ("Cb", 2 * h + 1)], writes=[pnk])

                        def fn(e, sT=sT, cs=cs):
                            e.matmul(psm[:, 128:129], lhsT=STm[sT][:], rhs=ones_bf[:, 0:1], start=True, stop=False)
                            e.matmul(psm[:, 128:129], lhsT=qk_[:, 0, cs], rhs=nb[:, 2 * h:2 * h + 1], start=False, stop=False)
                            return e.matmul(psm[:, 128:129], lhsT=qk_[:, 1, cs], rhs=nb[:, 2 * h + 1:2 * h + 2],
                                            start=False, stop=True)
                        P.add("pe", fn, reads=[("STm", sT), QK[0], QK[1], "nb"] + CONST, writes=["psm"])
                        P.add("act", lambda e, i=i: e.activation(
                            out=dtmp[:, 0:1], in_=psm[:, 128:129], func=AF.Abs, scale=ehT[:, i, h:h + 1]),
                            reads=["psm", ("ehT", i)], writes=["dtmp0"])
                        P.add("dve", lambda e: e.tensor_scalar(out=dtmp[:, 3:4], in0=dtmp[:, 0:1], scalar1=1.0,
                                                               scalar2=None, op0=ALU.max),
                              reads=["dtmp0"], writes=["dtmp3"])
                        P.add("dve", lambda e: e.reciprocal(out=dtmp[:, 1:2], in_=dtmp[:, 3:4]),
                              reads=["dtmp3"], writes=["dtmp1"])
                        P.add("dve", lambda e, i=i: e.tensor_tensor(out=dtmp[:, 2:3], in0=dtmp[:, 1:2],
                                                                   in1=ehT[:, i, h:h + 1], op=ALU.mult),
                              reads=["dtmp1", ("ehT", i)], writes=["dtmp2"])
                        P.add("dve", lambda e, i=i, pn=pn: e.scalar_tensor_tensor(
                            out=hb[:, i, :], in0=pn[:, :], scalar=dtmp[:, 2:3], in1=sgo_[:, i, :],
                            op0=ALU.mult, op1=ALU.mult),
                            reads=[pnk, "dtmp2", ("sgo", s_, i)], writes=[("hb", i)])
                        P.add("act", lambda e, i=i: e.activation(out=junk[:, 0:512], in_=hb[:, i, :], func=AF.Square,
                                                                 accum_out=ssq[:, i:i + 1]),
                              reads=[("hb", i)], writes=["junk", ("ssq", i)])
                        for c in range(2):
                            pd, pdk = pa_next()
                            P.add("pe", lambda e, kt=kt, c=c, i=i, pd=pd: e.matmul(
                                pd[:, :], lhsT=ktok[kt][:, c * 128:(c + 1) * 128], rhs=vm_[:, i, :], start=True, stop=True),
                                reads=[("ktok", kt), ("vm", s_, i)], writes=[pdk])
                            P.add("dve", lambda e, c=c, i=i, pd=pd: e.scalar_tensor_tensor(
                                out=Cf[:, 2 * h + c, :], in0=Cf[:, 2 * h + c, :], scalar=decT[:, i, h:h + 1],
                                in1=pd[:, :], op0=ALU.mult, op1=ALU.add),
                                reads=[pdk, ("Cf", 2 * h + c), ("decT", i)], writes=[("Cf", 2 * h + c)])
                            P.add("act", lambda e, c=c: e.copy(out=Cb[:, 2 * h + c, :], in_=Cf[:, 2 * h + c, :]),
                                  reads=[("Cf", 2 * h + c)], writes=[("Cb", 2 * h + c)])

                        def fn(e, kt=kt):
                            e.matmul(psm[:, 192:193], lhsT=ktok[kt][:, 0:128], rhs=ones_bf[:, 0:1], start=True, stop=True)
                            return e.matmul(psm[:, 193:194], lhsT=ktok[kt][:, 128:256], rhs=ones_bf[:, 0:1],
                                            start=True, stop=True)
                        P.add("pe", fn, reads=[("ktok", kt), "dtmp0"] + CONST, writes=["psm"])
                        P.add("dve", lambda e, i=i: e.scalar_tensor_tensor(
                            out=nf[:, 2 * h:2 * h + 2], in0=nf[:, 2 * h:2 * h + 2], scalar=decT[:, i, h:h + 1],
                            in1=psm[:, 192:194], op0=ALU.mult, op1=ALU.add),
                            reads=["psm", "nf", ("decT", i)], writes=["nf"])
                        P.add("dve", lambda e: e.tensor_copy(out=nb[:, 2 * h:2 * h + 2], in_=nf[:, 2 * h:2 * h + 2]),
                              reads=["nf"], writes=["nb"])
                        fill(per_tile)
                    P.add("pool", lambda e: e.tensor_scalar(out=ssq[:, 4:4 + nt], in0=ssq[:, 0:0 + nt],
                                                            scalar1=1.0 / 512, scalar2=EPS, op0=ALU.mult, op1=ALU.add),
                          reads=[("ssq", i) for i in range(nt)], writes=["ssq_s"])
                    P.add("pool", lambda e: e.tensor_tensor(out=ssq[:, 4:4 + nt], in0=ssq[:, 4:4 + nt],
                                                            in1=mhalf[:, 0:nt], op=ALU.pow),
                          reads=["ssq_s"] + CONST, writes=["ssq_s"])
                    hns = []
                    for i in range(nt):
                        hn_ = nxt("hbn", 2)
                        hns.append(hn_)
                        P.add("act", lambda e, hn_=hn_, i=i: e.activation(out=hbn[hn_][:], in_=hb[:, i, :], func=AF.Copy,
                                                                       scale=ssq[:, 4 + i:5 + i]),
                              reads=[("hb", i), "ssq_s"], writes=[("hbn", hn_)])
                    fill(10 ** 6)
                    for i in range(nt):
                        hn_ = hns[i]

                        def fn(e, hn_=hn_):
                            r = None
                            for c in range(4):
                                r = e.transpose(out=ptr[:, 4 + c, :], in_=hbn[hn_][:, c * 128:(c + 1) * 128],
                                                identity=ident[:])
                            return r
                        P.add("pe", fn, reads=[("hbn", hn_)] + CONST, writes=["ptr"])
                        P.add("dve", lambda e, i=i: e.tensor_tensor(
                            out=hbT[:, 4 * h:4 * h + 4, i * 128:(i + 1) * 128], in0=ptr[:, 4:8, :],
                            in1=smzg_[:, 0:4, i * 128:(i + 1) * 128], op=ALU.mult),
                            reads=["ptr"] + [("smzg", s_, c) for c in range(4)], writes=["hbT"])

                fox_proj(0)
                ensure_loads(0, 0, 0)
                for h in range(8):
                    if h + 1 < 8:
                        fox_proj(h + 1)
                    if h == 7:
                        for it in ml_items(0):
                            it()
                    fox_attn(h)
                for h in range(4):
                    filler = ml_items(h + 1) if h < 3 else gate_items(0)
                    ml_head(h, filler)


            def stageE():
                for hf in range(2):
                    uga, ugb, uwa, uwb0, uwb1 = U["half"][hf]
                    if hf == 1:
                        for it in gate_items(1):
                            it()
                    WAa, wak = w_get(uwa)
                    WB0, wb0k = w_get(uwb0, uwa)
                    WB1, wb1k = w_get(uwb1, uwa)
                    for mm in range(4):
                        m = hf * 4 + mm
                        pya, pyak = pa_next()
                        proj_feat(WAa, wak, mm * 128, xaT, "xaT", 8, NT, pya, pyak)
                        P.add("dve", lambda e, mm=mm, pya=pya, m=m: e.tensor_tensor(
                            out=etmpA[mm][:, 0:NT], in0=pya[:, 0:NT], in1=sga[:, m, 0:NT], op=ALU.mult),
                            reads=[pyak, ("sga", m)], writes=[("etmpA", mm)])
                    for mm in range(4):
                        m = hf * 4 + mm
                        pyb, pybk = pa_next()

                        def fn(e, mm=mm, pyb=pyb, WB0=WB0, WB1=WB1):
                            r = None
                            for k in range(16):
                                Wk = WB0 if k < 8 else WB1
                                r = e.matmul(pyb[:, 0:NT], lhsT=Wk[:, k % 8, mm * 128:(mm + 1) * 128], rhs=hbT[:, k, 0:NT],
                                             start=(k == 0), stop=(k == 15))
                            return r
                        P.add("pe", fn, reads=[wb0k, wb1k, "hbT"], writes=[pybk])
                        t1 = nxt("etmp", 2)
                        P.add("dve", lambda e, t1=t1, pyb=pyb, m=m: e.tensor_tensor(
                            out=etmp[t1][:, 0:NT], in0=pyb[:, 0:NT], in1=sgb[:, m, 0:NT], op=ALU.mult),
                            reads=[pybk, ("sgb", m)], writes=[("etmp", t1)])
                        P.add("dve", lambda e, mm=mm, t1=t1, m=m: e.tensor_tensor(
                            out=mrg[:, m, 0:NT], in0=etmpA[mm][:, 0:NT], in1=etmp[t1][:, 0:NT], op=ALU.add),
                            reads=[("etmpA", mm), ("etmp", t1)], writes=["mrg"])
                WO0, wo0k = w_get(U["wo"][0])
                WO1, wo1k = w_get(U["wo"][1], U["wo"][0])
                pouts = []
                for i, T in enumerate(tiles):
                    pp = []
                    for j, (WO, wok) in enumerate(((WO0, wo0k), (WO1, wo1k))):
                        po, pok = pa_next()

                        def fn(e, i=i, WO=WO, po=po):
                            r = None
                            for k in range(8):
                                r = e.matmul(po[:, :], lhsT=mrg[:, k, i * 128:(i + 1) * 128], rhs=WO[:, k, :],
                                             start=(k == 0), stop=(k == 7))
                            return r
                        P.add("pe", fn, reads=[wok, "mrg"], writes=[pok])
                        P.add("act", lambda e, po=po, i=i, j=j: e.activation(
                            out=junk[:, 0:512], in_=po[:, :], func=AF.Square, accum_out=ssE[:, 2 * i + j:2 * i + j + 1]),
                            reads=[pok], writes=["junk", ("ssE", 2 * i + j)])
                        pp.append((po, pok))
                    pouts.append(pp)
                    P.add("dve", lambda e, i=i: e.tensor_tensor(out=ssE[:, 4 + i:5 + i], in0=ssE[:, 2 * i:2 * i + 1],
                                                               in1=ssE[:, 2 * i + 1:2 * i + 2], op=ALU.add),
                          reads=[("ssE", 2 * i), ("ssE", 2 * i + 1)], writes=[("ssE_t", i)])
                P.add("pool", lambda e: e.tensor_scalar(out=ssE[:, 6:6 + nt], in0=ssE[:, 4:4 + nt],
                                                        scalar1=1.0 / D, scalar2=EPS, op0=ALU.mult, op1=ALU.add),
                      reads=[("ssE_t", i) for i in range(nt)], writes=["ssE_s"])
                P.add("pool", lambda e: e.tensor_tensor(out=ssE[:, 6:6 + nt], in0=ssE[:, 6:6 + nt],
                                                        in1=mhalf[:, 0:nt], op=ALU.pow),
                      reads=["ssE_s"] + CONST, writes=["ssE_s"])
                for i, T in enumerate(tiles):
                    xr = nxt("xres", 2)
                    ot = nxt("otile", 2)
                    P.add("sp", lambda e, xr=xr, T=T: e.dma_start(out=xres[xr][:], in_=H_in[T * 128:(T + 1) * 128, :]),
                          reads=[(hin_name, T)], writes=[("xres", xr)], dma_key=("xres", xr))
                    for j in range(2):
                        po, pok = pouts[i][j]
                        P.add("dve", lambda e, ot=ot, po=po, i=i, j=j: e.scalar_tensor_tensor(
                            out=otile[ot][:, j * 512:(j + 1) * 512], in0=po[:, :], scalar=ssE[:, 6 + i:7 + i],
                            in1=gpost[:, j * 512:(j + 1) * 512], op0=ALU.mult, op1=ALU.mult),
                            reads=[pok, "ssE_s", "gpost"], writes=[("otile", ot, j)])
                    P.add("dve", lambda e, ot=ot, xr=xr: e.tensor_tensor(out=otile[ot][:], in0=otile[ot][:],
                                                                         in1=xres[xr][:], op=ALU.add),
                          reads=[("otile", ot, 0), ("otile", ot, 1), ("xres", xr)],
                          writes=[("otile", ot, 0), ("otile", ot, 1)])
                    P.add("sp", lambda e, ot=ot, T=T: e.dma_start(out=H_out[T * 128:(T + 1) * 128, :], in_=otile[ot][:]),
                          reads=[("otile", ot, 0), ("otile", ot, 1)], writes=[(hout_name, T)], dma_key=("ost", ot))

            return stageA, body, stageE

        return pre_A, pre_body, [make_group(g, tiles, (par0 + g) % 2) for g, tiles in enumerate(groups)]

    layers = []
    par0 = 0
    for l in range(depth):
        H_in, hin = (x_in, "Hx") if l == 0 else (Hs, "Hs")
        H_out, hout = (out_d, "Hout") if l == depth - 1 else (Hs, "Hs")
        layers.append(layer(l, H_in, hin, H_out, hout, par0))
        par0 = (par0 + len(groups)) % 2
    steps = []
    for l in range(depth):
        pre_A, pre_body, gl = layers[l]
        cch = conv_chunks(l + 1, len(gl)) if l + 1 < depth else [None] * len(gl)
        for g, (sA, bd, sE) in enumerate(gl):
            def bd2(bd=bd, cc=cch[g]):
                if cc is not None:
                    cc()
                bd()
            steps.append((g, pre_A, pre_body, sA, bd2, sE))
    steps[0][1]()
    steps[0][3]()
    for idx, (g, pre_A, pre_body, sA, bd, sE) in enumerate(steps):
        if g == 0:
            pre_body()
        bd()
        if idx + 1 < len(steps):
            ng, npre_A, _, nsA, _, _ = steps[idx + 1]
            if ng == 0:
                npre_A()
            nsA()
        sE()

    P.emit(nc, st, final_keys=[("ost", 0), ("ost", 1)])
    st.close()
    return nc


def prep_inputs(x, meta_tokens, norm_pre, norm_post, w_in, b_fox_f, conv_w, conv_b, b_mlstm_i, b_mlstm_f,
                mlstm_head_norm, w_a, w_b, w_o, ntiles, depth):
    NTOK = ntiles * 128
    B = x.shape[0]
    f32 = np.float32
    gpre = np.ascontiguousarray(np.broadcast_to(np.asarray(norm_pre, f32)[:depth, None, :], (depth, 128, D)))
    gpost = np.ascontiguousarray(np.broadcast_to(np.asarray(norm_post, f32)[:depth, None, :], (depth, 128, D)))
    small = np.zeros((128, depth, 112), f32)
    for l in range(depth):
        gb = np.concatenate([np.asarray(b_fox_f[l], f32), np.asarray(b_mlstm_i[l], f32), np.asarray(b_mlstm_f[l], f32)])
        small[:, l, 0:16] = gb[None, :]
        cw = np.asarray(conv_w[l], f32).reshape(4, 16, 128).transpose(2, 1, 0)
        small[:, l, 16:80] = cw.reshape(128, 64)
        small[:, l, 80:96] = np.asarray(conv_b[l], f32).reshape(16, 128).T
        small[:, l, 96:112] = np.asarray(mlstm_head_norm[l], f32).reshape(16, 128).T
    shared = {
        "gpre": gpre, "gpost": gpost, "small": small,
        "w_in": np.ascontiguousarray(np.asarray(w_in, f32)[:depth]),
        "w_a": np.ascontiguousarray(np.asarray(w_a, f32)[:depth]),
        "w_b": np.ascontiguousarray(np.asarray(w_b, f32)[:depth]),
        "w_o": np.ascontiguousarray(np.asarray(w_o, f32)[:depth]),
    }
    maps = []
    for b in range(B):
        xp = np.zeros((NTOK, D), f32)
        seq = np.concatenate([np.asarray(meta_tokens, f32), np.asarray(x[b], f32)], axis=0)
        n = min(NTOK, seq.shape[0])
        xp[:n] = seq[:n]
        m = dict(shared)
        m["x"] = xp
        maps.append(m)
    return maps


def run(inputs, ntiles, depth, trace=False):
    nc = build_program(ntiles, depth)
    maps = prep_inputs(ntiles=ntiles, depth=depth, **inputs)
    res = run_bass_kernel_spmd(nc, maps, core_ids=list(range(len(maps))), trace=trace)
    outs = [r["out"] for r in res.results]
    return np.stack(outs, axis=0), res


def kernel(**inputs):
    ntiles = (NMETA + SEQ + 127) // 128
    out, _ = run(inputs, ntiles, DEPTH)
    return np.ascontiguousarray(out[:, NMETA:NMETA + SEQ, :]).astype(np.float32)
```

```python
import os
import numpy as np
import ml_dtypes
from contextlib import ExitStack
import concourse.bass as bass
import concourse.mybir as mybir
from concourse.bass_utils import run_bass_kernel_spmd

F32 = mybir.dt.float32
BF16 = mybir.dt.bfloat16
AF = mybir.ActivationFunctionType
ALU = mybir.AluOpType

D = 1024
DEPTH = 4
NMETA = 16
SEQ = 4096
NIN = 14352
GT = 2
KSEG = 4
NKS = 6
KDIST = 4
NW = 4
C_FQ, C_FK, C_FV, C_FF, C_FZ = 0, 1024, 2048, 3072, 3080
C_MQK, C_MV, C_MI, C_MF, C_MO, C_MZ, C_GA, C_GB = 4104, 6152, 8200, 8204, 8208, 10256, 12304, 13328
EPS = 1e-6


class Op:
    __slots__ = ("eng", "fn", "deps", "dma_key", "ndma", "target", "flag", "is_dma")


class _FirstRec:
    def __init__(self, e):
        self._e = e
        self.first = None

    def __getattr__(self, name):
        attr = getattr(self._e, name)
        if not callable(attr):
            return attr

        def wrapped(*a, **k):
            r = attr(*a, **k)
            if self.first is None:
                self.first = r
            return r
        return wrapped


class Prog:
    ENGS = ("pe", "act", "dve", "pool", "sp")

    def __init__(self):
        self.ops = []
        self.last_w = {}
        self.readers = {}
        self.last_dma = {}
        self.dma_count = {}
        self.ops_index = {}

    def add(self, eng, fn, reads=(), writes=(), dma_key=None, ndma=1):
        op = Op()
        op.eng = eng
        op.fn = fn
        op.dma_key = dma_key
        op.is_dma = dma_key is not None
        op.ndma = ndma
        op.flag = 0
        op.target = 0
        deps = set()
        for r in reads:
            w = self.last_w.get(r)
            if w is not None:
                deps.add(w)
        for w_ in writes:
            lw = self.last_w.get(w_)
            if lw is not None:
                deps.add(lw)
            rd = self.readers.get(w_)
            if rd:
                deps.update(rd.values())
        for r in reads:
            d = self.readers.setdefault(r, {})
            d[("dma", dma_key) if op.is_dma else eng] = op
        for w_ in writes:
            self.last_w[w_] = op
            self.readers[w_] = {}
        if op.is_dma:
            prev = self.last_dma.get(dma_key)
            if prev is not None:
                deps.add(prev)
            self.last_dma[dma_key] = op
            n = self.dma_count.get(dma_key, 0) + ndma
            self.dma_count[dma_key] = n
            op.target = 16 * n
        deps.discard(op)
        op.deps = deps
        self.ops_index[op] = len(self.ops)
        self.ops.append(op)
        return op

    def emit(self, nc, stack, final_keys):
        idx = {}
        cnt_ = {}
        for op in self.ops:
            k = ("dma", op.dma_key) if op.is_dma else op.eng
            cnt_[k] = cnt_.get(k, 0) + 1
            idx[op] = (k, cnt_[k])
        K = {}
        last_on_eng = {}
        n_before = n_after = 0
        for op in self.ops:
            prev = last_on_eng.get(op.eng)
            cur = dict(K[prev]) if prev is not None else {}
            cand = []
            for d in op.deps:
                if (not d.is_dma) and d.eng == "pe" and op.eng == "pe" and not op.is_dma:
                    continue
                cand.append(d)
            n_before += len(cand)
            cand.sort(key=lambda d: -self.ops_index[d])
            keep = []
            for d in cand:
                k, i = idx[d]
                if cur.get(k, 0) >= i:
                    continue
                keep.append(d)
                for kk, vv in K[d].items():
                    if cur.get(kk, 0) < vv:
                        cur[kk] = vv
                if cur.get(k, 0) < i:
                    cur[k] = i
            n_after += len(keep)
            op.deps = set(keep)
            K[op] = cur
            last_on_eng[op.eng] = op
        print("[sync] dependency edges: %d -> %d after transitive reduction" % (n_before, n_after))
        need_flag = set()
        for op in self.ops:
            for d in op.deps:
                if d.is_dma:
                    continue
                if d.eng == "pe" and op.eng == "pe" and not op.is_dma:
                    continue
                need_flag.add(d)
        cnt = {e: 0 for e in self.ENGS}
        for op in self.ops:
            if (not op.is_dma) and op in need_flag:
                cnt[op.eng] += 1
                op.flag = cnt[op.eng]
        esem = {e: stack.enter_context(nc.semaphore("s_" + e)) for e in self.ENGS}
        dsem = {}
        for i, k in enumerate(self.dma_count):
            dsem[k] = stack.enter_context(nc.semaphore("d%d" % i))
        per_eng = {e: [] for e in self.ENGS}
        for op in self.ops:
            per_eng[op.eng].append(op)
        block = stack.enter_context(nc.Block())
        final = [(dsem[k], 16 * self.dma_count[k]) for k in final_keys]

        def run(eng_name, e):
            waited = {}
            for op in per_eng[eng_name]:
                waits = {}
                for d in op.deps:
                    if d.is_dma:
                        s, v = dsem[d.dma_key], d.target
                    else:
                        if d.eng == "pe" and eng_name == "pe" and not op.is_dma:
                            continue
                        s, v = esem[d.eng], d.flag
                    if waits.get(s, 0) < v:
                        waits[s] = v
                pending = [(s, v) for s, v in waits.items() if waited.get(s, 0) < v]
                for s, v in pending[:-1]:
                    e.wait_ge(s, v)
                    waited[s] = v
                rec = _FirstRec(e)
                r = op.fn(rec)
                if pending:
                    s, v = pending[-1]
                    assert rec.first is not None
                    rec.first._wait_ge(s, v)
                    waited[s] = v
                if op.is_dma:
                    if not isinstance(r, (list, tuple)):
                        r = [r]
                    assert len(r) == op.ndma
                    for ins in r:
                        ins.then_inc(dsem[op.dma_key], 16)
                elif op.flag:
                    r.then_inc(esem[eng_name], 1)
            if eng_name == "sp":
                for s, v in final:
                    e.wait_ge(s, v)

        @block.tensor
        def _(e):
            run("pe", e)

        @block.scalar
        def _(e):
            run("act", e)

        @block.vector
        def _(e):
            run("dve", e)

        @block.gpsimd
        def _(e):
            run("pool", e)

        @block.sync
        def _(e):
            run("sp", e)


def build_program(ntiles, depth):
    NTOK = ntiles * 128
    groups = [list(range(t, min(t + GT, ntiles))) for t in range(0, ntiles, GT)]
    NTM = GT * 128
    nc = bass.Bass("TRN2", target_bir_lowering=False)
    P = Prog()
    st = ExitStack()

    def dram(name, shape, dt, kind="Internal"):
        return nc.dram_tensor(name, shape, dt, kind=kind).ap()

    x_in = dram("x", [NTOK, D], F32, "ExternalInput")
    gpre_d = dram("gpre", [depth, 128, D], F32, "ExternalInput")
    gpost_d = dram("gpost", [depth, 128, D], F32, "ExternalInput")
    small_d = dram("small", [128, depth, 112], F32, "ExternalInput")
    w_in_d = dram("w_in", [depth, D, NIN], F32, "ExternalInput")
    w_a_d = dram("w_a", [depth, D, D], F32, "ExternalInput")
    w_b_d = dram("w_b", [depth, 2 * D, D], F32, "ExternalInput")
    w_o_d = dram("w_o", [depth, D, D], F32, "ExternalInput")
    out_d = dram("out", [NTOK, D], F32, "ExternalOutput")
    Hs = dram("Hs", [NTOK, D], F32)
    KTd = dram("KTd", [8, 128, NTOK], BF16)
    Vd = dram("Vd", [8, 128, ntiles, 128], BF16)
    Wb = dram("Wb", [depth, 39, 128, 8, 512], BF16)

    def sb(name, shape, dt):
        return st.enter_context(nc.sbuf_tensor("sb_" + name, shape, dt))

    def ps(name, shape, dt):
        return st.enter_context(nc.psum_tensor("ps_" + name, shape, dt))

    ident = sb("ident", [128, 128], BF16)
    mask01 = sb("mask01", [128, 128], BF16)
    maskneg = sb("maskneg", [128, 128], BF16)
    ones_bf = sb("ones_bf", [128, 128], BF16)
    tri_f = sb("tri_f", [128, 128], F32)
    ones_f = sb("ones_f", [128, 128], F32)
    cst = sb("cst", [128, 4], F32)
    mhalf = sb("mhalf", [128, 8], F32)
    small = sb("small", [128, depth, 112], F32)
    gpre = sb("gpre_t", [128, D], F32)
    gpost = sb("gpost_t", [128, D], F32)
    xt = [sb("xt%d" % i, [128, D], F32) for i in range(2)]
    junk = sb("junk", [128, D], BF16)
    ssA = sb("ssA", [128, 8], F32)
    xb = [sb("xb%d" % i, [128, D], BF16) for i in range(2)]
    hnTb = [sb("hnT%d" % i, [128, 8, NTM], BF16) for i in range(2)]
    Wsl = [sb("W%d" % i, [128, 8, 512], BF16) for i in range(NW)]
    vnew = sb("vnew", [128, GT, 8, 128], BF16)
    qT = [sb("qT%d" % i, [128, NTM], BF16) for i in range(2)]
    kTn = [sb("kTn%d" % i, [128, NTM], BF16) for i in range(2)]
    sfz = [sb("sfz%d" % i, [128, NTM], BF16) for i in range(2)]
    KTs = [sb("KTs%d" % i, [128, KSEG * 128], BF16) for i in range(NKS)]
    Vs = [sb("Vs%d" % i, [128, KSEG, 128], BF16) for i in range(NKS)]
    PT = [sb("PT%d" % i, [128, NTM], BF16) for i in range(4)]
    rl = [sb("rl%d" % i, [128, NTM], F32) for i in range(2)]
    Osb = [sb("Osb%d" % i, [128, NTM], F32) for i in range(2)]
    xaT = sb("xaT", [128, 8, NTM], BF16)
    Fcum = sb("Fcum", [128, ntiles, 8], F32)
    carry = sb("carry", [128, 8], F32)
    Fref = sb("Fref", [128, 8], F32)
    biasH = [sb("biasH%d" % i, [128, ntiles], F32) for i in range(2)]
    gz = sb("gz", [128, 16], F32)
    ge = sb("ge", [128, 16], F32)
    gsp = sb("gsp", [128, 16], F32)
    lg = sb("lg", [128, 12], F32)
    bloc = sb("bloc", [128, GT, 4], F32)
    blast = sb("blast", [128, GT, 4], F32)
    gtmp = sb("gtmp", [128, GT, 8], F32)
    gsT = sb("gsT", [128, GT, 4], F32)
    gdsT = sb("gdsT", [128, GT, 4], F32)
    ehT = sb("ehT", [128, GT, 4], F32)
    decT = sb("decT", [128, GT, 4], F32)
    stage = [sb("stage%d" % i, [128, NTM + 3], BF16) for i in range(3)]
    hist = sb("hist", [128, 16, 3], BF16)
    cdiag = sb("cdiag", [128, 64, 128], BF16)
    qkTb = [sb("qkT%d" % i, [128, 4, NTM], BF16) for i in range(2)]
    smz = [sb("smz%d" % i, [128, NTM], BF16) for i in range(2)]
    smzgb = [sb("smzg%d" % i, [128, 4, NTM], BF16) for i in range(2)]
    vmb = [sb("vm%d" % i, [128, GT, 512], BF16) for i in range(2)]
    sgob = [sb("sgo%d" % i, [128, GT, 512], BF16) for i in range(2)]
    ktok = [sb("ktok%d" % i, [128, 256], BF16) for i in range(2)]
    STm = [sb("STm%d" % i, [128, 128], BF16) for i in range(2)]
    dtmp = sb("dtmp", [128, 8], F32)
    hb = sb("hb", [128, GT, 512], F32)
    ssq = sb("ssq", [128, 8], F32)
    hbn = [sb("hbn%d" % i, [128, 512], BF16) for i in range(2)]
    hbT = sb("hbT", [128, 16, NTM], BF16)
    Cf = sb("Cf", [128, 8, 512], F32)
    Cb = sb("Cb", [128, 8, 512], BF16)
    nf = sb("nf", [128, 8], F32)
    nb = sb("nb", [128, 8], BF16)
    sga = sb("sga", [128, 8, NTM], BF16)
    sgb = sb("sgb", [128, 8, NTM], BF16)
    etmp = [sb("etmp%d" % i, [128, NTM], F32) for i in range(2)]
    etmpA = [sb("etmpA%d" % i, [128, NTM], F32) for i in range(4)]
    mrg = sb("mrg", [128, 8, NTM], BF16)
    otile = [sb("otile%d" % i, [128, D], F32) for i in range(2)]
    xres = [sb("xres%d" % i, [128, D], F32) for i in range(2)]
    ssE = sb("ssE", [128, 8], F32)
    pa = [ps("pa%d" % i, [128, 512], F32) for i in range(4)]
    pO = ps("pO", [128, 512], F32)
    pL = ps("pL", [128, 512], F32)
    ptr = ps("ptr", [128, 8, 128], BF16)
    psm = ps("psm", [128, 512], F32)

    rot = {}

    def nxt(key, n):
        v = rot.get(key, 0)
        rot[key] = v + 1
        return v % n

    def c_memset(t, val):
        P.add("pool", lambda e, t=t, val=val: e.memset(t[:], val), writes=[("c", id(t))])

    def c_select(t, fill, keep_ge):
        if keep_ge:
            P.add("pool", lambda e, t=t: e.affine_select(out=t[:], in_=t[:], pattern=[[1, 128]],
                                                        compare_op=ALU.is_ge, fill=fill, base=0,
                                                        channel_multiplier=-1),
                  reads=[("c", id(t))], writes=[("c", id(t))])
        else:
            P.add("pool", lambda e, t=t: e.affine_select(out=t[:], in_=t[:], pattern=[[-1, 128]],
                                                        compare_op=ALU.not_equal, fill=fill, base=0,
                                                        channel_multiplier=1),
                  reads=[("c", id(t))], writes=[("c", id(t))])

    c_memset(ident, 0.0)
    c_select(ident, 1.0, False)
    c_memset(mask01, 1.0)
    c_select(mask01, 0.0, True)
    c_memset(maskneg, 0.0)
    c_select(maskneg, -30000.0, True)
    c_memset(ones_bf, 1.0)
    c_memset(tri_f, 1.0)
    c_select(tri_f, 0.0, True)
    c_memset(ones_f, 1.0)
    P.add("pool", lambda e: e.memset(mhalf[:], -0.5), writes=[("c", "mhalf")])
    P.add("pool", lambda e: e.memset(cst[:, 0:1], EPS), writes=[("c", "cst0")])
    P.add("pool", lambda e: e.memset(cst[:, 1:2], 1.0), writes=[("c", "cst1")])
    P.add("pool", lambda e: e.memset(cst[:, 2:3], -float(np.log(16.0))), writes=[("c", "cst2")])
    CONST = [("c", id(ident)), ("c", id(mask01)), ("c", id(maskneg)), ("c", id(ones_bf)),
             ("c", id(tri_f)), ("c", id(ones_f)), ("c", "cst0"), ("c", "cst1"), ("c", "cst2"), ("c", "mhalf")]
    P.add("sp", lambda e: e.dma_start(out=small[:], in_=small_d[:, :, :]), writes=["small"],
          dma_key="small")

    wstate = {"units": [], "issued": 0, "cur_layer": 0, "cur_uidx": 0}
    NUNITS = 39

    def w_unit(parts):
        wstate["units"].append((wstate["cur_layer"], wstate["cur_uidx"], parts))
        wstate["cur_uidx"] += 1
        return len(wstate["units"]) - 1

    def w_issue_upto(u):
        while wstate["issued"] <= u and wstate["issued"] < len(wstate["units"]):
            i = wstate["issued"]
            slot = i % NW
            l_, uidx, parts = wstate["units"][i]
            if l_ == 0:
                def fn(e, slot=slot, parts=parts):
                    r = []
                    for (c0, ncols, src) in parts:
                        r.append(e.dma_start(out=Wsl[slot][:, :, c0:c0 + ncols],
                                             in_=src.rearrange("(k p) n -> p k n", p=128)))
                    return r
                P.add("pool", fn, writes=[("W", slot)], dma_key=("W", slot), ndma=len(parts))
            else:
                used = max(c0 + ncols for (c0, ncols, _) in parts)

                def fn(e, slot=slot, l_=l_, uidx=uidx, used=used):
                    return [e.dma_start(out=Wsl[slot][:, :, 0:used], in_=Wb[l_, uidx, :, :, 0:used])]
                P.add("pool", fn, reads=[("Wb", l_)], writes=[("W", slot)], dma_key=("W", slot), ndma=1)
            wstate["issued"] += 1

    def conv_chunks(l_, nchunks):
        first = next(i for i, (ll, ui, _) in enumerate(wstate["units"]) if ll == l_ and ui == 0)
        dmas = []
        for (ll, uidx, parts) in wstate["units"][first:first + NUNITS]:
            assert ll == l_
            for (c0, ncols, src) in parts:
                dmas.append((uidx, c0, ncols, src))
        per = (len(dmas) + nchunks - 1) // nchunks
        chunks = []
        for ci in range(nchunks):
            sub = dmas[ci * per:(ci + 1) * per]
            if not sub:
                chunks.append(None)
                continue

            def emit(sub=sub):
                def fn(e):
                    return [e.dma_start(out=Wb[l_, uidx, :, :, c0:c0 + ncols],
                                        in_=src.rearrange("(k p) n -> p k n", p=128))
                            for (uidx, c0, ncols, src) in sub]
                P.add("pool", fn, writes=[("Wb", l_)], dma_key=("wconv", l_), ndma=len(sub))
            chunks.append(emit)
        return chunks

    def w_get(u, first_live=None):
        fl = u if first_live is None else first_live
        w_issue_upto(max(u, fl + NW - 1))
        slot = u % NW
        return Wsl[slot], ("W", slot)

    unit_plan = []

    def plan_layer(l):
        pl = {}
        wi = w_in_d
        per_group = []
        for g in range(len(groups)):
            u = {}
            wstate["cur_layer"] = l
            wstate["cur_uidx"] = 0
            u["fv"] = [w_unit([(0, 512, wi[l, :, C_FV + j * 512:C_FV + (j + 1) * 512])]) for j in range(2)]
            u["gate"] = w_unit([(0, 8, wi[l, :, C_FF:C_FF + 8]), (8, 8, wi[l, :, C_MI:C_MI + 8])])
            u["fox"] = []
            for h in range(8):
                u["fox"].append(w_unit([(0, 128, wi[l, :, C_FQ + h * 128:C_FQ + (h + 1) * 128]),
                                        (128, 128, wi[l, :, C_FK + h * 128:C_FK + (h + 1) * 128]),
                                        (256, 128, wi[l, :, C_FZ + h * 128:C_FZ + (h + 1) * 128])]))
            u["ml"] = []
            for h in range(4):
                ua = w_unit([(0, 256, wi[l, :, C_MQK + h * 256:C_MQK + (h + 1) * 256]),
                             (256, 256, wi[l, :, C_MQK + 1024 + h * 256:C_MQK + 1024 + (h + 1) * 256])])
                ub = w_unit([(0, 512, wi[l, :, C_MZ + h * 512:C_MZ + (h + 1) * 512])])
                uc = w_unit([(0, 512, wi[l, :, C_MV + h * 512:C_MV + (h + 1) * 512])])
                ud = w_unit([(0, 512, wi[l, :, C_MO + h * 512:C_MO + (h + 1) * 512])])
                u["ml"].append((ua, ub, uc, ud))
            u["half"] = []
            for hf in range(2):
                uga = w_unit([(0, 512, wi[l, :, C_GA + hf * 512:C_GA + (hf + 1) * 512])])
                ugb = w_unit([(0, 512, wi[l, :, C_GB + hf * 512:C_GB + (hf + 1) * 512])])
                uwa = w_unit([(0, 512, w_a_d[l, :, hf * 512:(hf + 1) * 512])])
                uwb0 = w_unit([(0, 512, w_b_d[l, 0:1024, hf * 512:(hf + 1) * 512])])
                uwb1 = w_unit([(0, 512, w_b_d[l, 1024:2048, hf * 512:(hf + 1) * 512])])
                u["half"].append((uga, ugb, uwa, uwb0, uwb1))
            u["wo"] = [w_unit([(0, 512, w_o_d[l, :, j * 512:(j + 1) * 512])]) for j in range(2)]
            assert wstate["cur_uidx"] == NUNITS
            per_group.append(u)
        pl["groups"] = per_group
        return pl

    for l in range(depth):
        unit_plan.append(plan_layer(l))

    def proj_feat(W, wkey, c0, rhs_t, rhs_key, nk, NT, dst_ps, dst_key, koff=0, extra_reads=()):
        def fn(e):
            r = None
            for k in range(nk):
                r = e.matmul(dst_ps[:, 0:NT], lhsT=W[:, k, c0:c0 + 128], rhs=rhs_t[:, koff + k, 0:NT],
                             start=(k == 0), stop=(k == nk - 1))
            return r
        P.add("pe", fn, reads=[wkey, rhs_key] + list(extra_reads), writes=[dst_key])

    def proj_tok(W, wkey, ncols, ti, dst_ps, dst_key, hnT, HK):
        def fn(e):
            r = None
            for k in range(8):
                r = e.matmul(dst_ps[:, 0:ncols], lhsT=hnT[:, k, ti * 128:(ti + 1) * 128],
                             rhs=W[:, k, 0:ncols], start=(k == 0), stop=(k == 7))
            return r
        P.add("pe", fn, reads=[wkey, HK], writes=[dst_key])

    pa_mode = {"n": 4}

    def pa_next():
        i = nxt("pa", pa_mode["n"])
        if i == 4:
            return pL, "pL"
        if i == 5:
            return pO, "pO"
        return pa[i], ("pa", i)

    def layer(l, H_in, hin_name, H_out, hout_name, par0):
        plan = unit_plan[l]
        sm = small[:, l, :]

        def pre_A():
            P.add("sp", lambda e: e.dma_start(out=gpre[:], in_=gpre_d[l, :, :]), writes=["gpre"], dma_key="gpre")

        def pre_body():
            P.add("sp", lambda e: e.dma_start(out=gpost[:], in_=gpost_d[l, :, :]), writes=["gpost"], dma_key="gpost")
            P.add("pool", lambda e: e.memset(Cf[:], 0.0), writes=[("Cf", i) for i in range(8)])
            P.add("pool", lambda e: e.memset(Cb[:], 0.0), writes=[("Cb", i) for i in range(8)])
            P.add("pool", lambda e: e.memset(nf[:], 0.0), writes=["nf"])
            P.add("pool", lambda e: e.memset(nb[:], 0.0), writes=["nb"])
            P.add("pool", lambda e: e.memset(hist[:], 0.0), writes=[("hist", i) for i in range(16)])
            P.add("pool", lambda e: e.memset(carry[:], 0.0), writes=["carry"])
            for j in range(64):
                P.add("dve", lambda e, j=j: e.tensor_scalar(out=cdiag[:, j, :], in0=ident[:], scalar1=sm[:, 16 + j:17 + j],
                                                          scalar2=None, op0=ALU.mult),
                      reads=["small"] + CONST, writes=["cdiag"])

        def make_group(g, tiles, par):
            U = plan["groups"][g]
            nt = len(tiles)
            NT = nt * 128
            T0 = tiles[0]
            tok0 = T0 * 128

            hnT = hnTb[par]
            HK = ("hnT", par)

            def gate_items(hf):
                uga, ugb = U["half"][hf][0], U["half"][hf][1]
                items = []

                def one(u, dst, dname, mm):
                    WG, wgk = w_get(u)
                    m = hf * 4 + mm
                    pc, pck = pa_next()
                    proj_feat(WG, wgk, mm * 128, hnT, HK, 8, NT, pc, pck)
                    P.add("act", lambda e: e.activation(out=dst[:, m, 0:NT], in_=pc[:, 0:NT], func=AF.Sigmoid),
                          reads=[pck], writes=[(dname, m)])
                for mm in range(4):
                    items.append(lambda mm=mm: one(uga, sga, "sga", mm))
                for mm in range(4):
                    items.append(lambda mm=mm: one(ugb, sgb, "sgb", mm))
                return items

            def stageA():
                for i, T in enumerate(tiles):
                    s = nxt("xt", 2)
                    P.add("sp", lambda e, s=s, T=T: e.dma_start(out=xt[s][:], in_=H_in[T * 128:(T + 1) * 128, :]),
                          reads=[(hin_name, T)], writes=[("xt", s)], dma_key=("xt", s))
                    P.add("act", lambda e, s=s, i=i: e.activation(out=junk[:], in_=xt[s][:], func=AF.Square,
                                                                accum_out=ssA[:, i:i + 1]),
                          reads=[("xt", s)], writes=["junk", ("ssA", i)])
                P.add("pool", lambda e: e.tensor_scalar(out=ssA[:, 4:4 + nt], in0=ssA[:, 0:0 + nt],
                                                        scalar1=1.0 / D, scalar2=EPS, op0=ALU.mult, op1=ALU.add),
                      reads=[("ssA", i) for i in range(nt)], writes=["ssA_s"])
                P.add("pool", lambda e: e.tensor_tensor(out=ssA[:, 4:4 + nt], in0=ssA[:, 4:4 + nt],
                                                        in1=mhalf[:, 0:nt], op=ALU.pow),
                      reads=["ssA_s"] + CONST, writes=["ssA_s"])
                base_slot = (rot["xt"] - nt) % 2
                for i, T in enumerate(tiles):
                    s = (base_slot + i) % 2
                    b = nxt("xb", 2)
                    P.add("dve", lambda e, s=s, b=b, i=i: e.scalar_tensor_tensor(
                        out=xb[b][:], in0=xt[s][:], scalar=ssA[:, 4 + i:5 + i], in1=gpre[:],
                        op0=ALU.mult, op1=ALU.mult),
                        reads=[("xt", s), "ssA_s", "gpre"], writes=[("xb", b)])

                    def fn(e, b=b):
                        r = None
                        for k in range(8):
                            r = e.transpose(out=ptr[:, k, :], in_=xb[b][:, k * 128:(k + 1) * 128], identity=ident[:])
                        return r
                    P.add("pe", fn, reads=[("xb", b)] + CONST, writes=["ptr"])
                    P.add("dve", lambda e, i=i: e.tensor_copy(out=hnT[:, :, i * 128:(i + 1) * 128], in_=ptr[:, :, :]),
                          reads=["ptr"], writes=[HK])


            def body():
                for j in range(2):
                    W, wk = w_get(U["fv"][j])
                    for i, T in enumerate(tiles):
                        pt_, pk = pa_next()
                        proj_tok(W, wk, 512, i, pt_, pk, hnT, HK)
                        P.add("act", lambda e, i=i, j=j, pt_=pt_: e.copy(
                            out=vnew[:, i, j * 4:(j + 1) * 4, :], in_=pt_[:, :].rearrange("p (h d) -> p h d", h=4)),
                            reads=[pk], writes=[("vnew", i)])
                for i, T in enumerate(tiles):
                    P.add("sp", lambda e, i=i, T=T: e.dma_start(out=Vd[:, :, T, :].rearrange("h p d -> p h d"),
                                                              in_=vnew[:, i, :, :]),
                          reads=[("vnew", i)], writes=["Vd"], dma_key=("vst", i))
                W, wk = w_get(U["gate"])
                for i, T in enumerate(tiles):
                    proj_tok(W, wk, 16, i, psm, "psm", hnT, HK)
                    P.add("dve", lambda e: e.tensor_tensor(out=gz[:], in0=psm[:, 0:16], in1=sm[:, 0:16], op=ALU.add),
                          reads=["psm", "small"], writes=["gz"])
                    P.add("act", lambda e: e.activation(out=ge[:], in_=gz[:], func=AF.Exp, scale=-1.0),
                          reads=["gz"], writes=["ge"])
                    P.add("act", lambda e: e.activation(out=gsp[:], in_=ge[:], func=AF.Ln, bias=cst[:, 1:2]),
                          reads=["ge"] + CONST, writes=["gsp"])
                    P.add("dve", lambda e: e.tensor_scalar(out=lg[:, 0:8], in0=gsp[:, 0:8], scalar1=-1.0, scalar2=None,
                                                           op0=ALU.mult), reads=["gsp"], writes=["lg0"])
                    P.add("dve", lambda e: e.tensor_scalar(out=lg[:, 8:12], in0=gsp[:, 12:16], scalar1=-1.0,
                                                           scalar2=None, op0=ALU.mult), reads=["gsp"], writes=["lg1"])
                    P.add("pe", lambda e: e.matmul(psm[:, 32:44], lhsT=tri_f[:], rhs=lg[:], start=True, stop=True),
                          reads=["lg0", "lg1"] + CONST, writes=["psm"])
                    P.add("dve", lambda e, T=T: e.tensor_tensor(out=Fcum[:, T, :], in0=psm[:, 32:40], in1=carry[:],
                                                               op=ALU.add),
                          reads=["psm", "carry"], writes=["Fcum"])
                    P.add("dve", lambda e, i=i: e.tensor_copy(out=bloc[:, i, :], in_=psm[:, 40:44]),
                          reads=["psm"], writes=[("bloc", i)])
                    P.add("pe", lambda e: e.matmul(psm[:, 64:76], lhsT=ones_f[:], rhs=lg[:], start=True, stop=True),
                          reads=["lg0", "lg1", "Fcum", ("bloc", i)] + CONST, writes=["psm"])
                    P.add("dve", lambda e: e.tensor_tensor(out=carry[:], in0=psm[:, 64:72], in1=carry[:], op=ALU.add),
                          reads=["psm", "carry", "Fcum"], writes=["carry"])
                    P.add("dve", lambda e, i=i: e.tensor_copy(out=blast[:, i, :], in_=psm[:, 72:76]),
                          reads=["psm"], writes=[("blast", i)])
                    if i == 0:
                        P.add("dve", lambda e: e.tensor_copy(out=Fref[:], in_=carry[:]), reads=["carry"], writes=["Fref"])
                    P.add("dve", lambda e, i=i: e.tensor_tensor(out=gtmp[:, i, 0:4], in0=gz[:, 8:12], in1=bloc[:, i, :],
                                                               op=ALU.subtract),
                          reads=["gz", ("bloc", i)], writes=[("gtmp0", i)])
                    P.add("dve", lambda e, i=i: e.tensor_tensor(out=gtmp[:, i, 4:8], in0=gtmp[:, i, 0:4],
                                                               in1=blast[:, i, :], op=ALU.add),
                          reads=[("gtmp0", i), ("blast", i)], writes=[("gtmp1", i)])
                    P.add("act", lambda e, i=i: e.activation(out=gsT[:, i, :], in_=gtmp[:, i, 0:4], func=AF.Exp),
                          reads=[("gtmp0", i)], writes=[("gsT", i)])
                    P.add("act", lambda e, i=i: e.activation(out=gdsT[:, i, :], in_=gtmp[:, i, 4:8], func=AF.Exp),
                          reads=[("gtmp1", i)], writes=[("gdsT", i)])
                    P.add("act", lambda e, i=i: e.activation(out=ehT[:, i, :], in_=bloc[:, i, :], func=AF.Exp,
                                                             bias=cst[:, 2:3]),
                          reads=[("bloc", i)] + CONST, writes=[("ehT", i)])
                    P.add("act", lambda e, i=i: e.activation(out=decT[:, i, :], in_=blast[:, i, :], func=AF.Exp),
                          reads=[("blast", i)], writes=[("decT", i)])


                nkb = tiles[-1] + 1

                def fox_proj(h):
                    W, wk = w_get(U["fox"][h])
                    hb_ = h % 2
                    pq, pqk = pa_next()
                    proj_feat(W, wk, 0, hnT, HK, 8, NT, pq, pqk)
                    P.add("dve", lambda e, hb_=hb_, pq=pq: e.tensor_scalar(
                        out=qT[hb_][:, 0:NT], in0=pq[:, 0:NT], scalar1=float(128 ** -0.5), scalar2=None, op0=ALU.mult),
                        reads=[pqk], writes=[("qT", hb_)])
                    pk_, pkk = pa_next()
                    proj_feat(W, wk, 128, hnT, HK, 8, NT, pk_, pkk)
                    P.add("dve", lambda e, hb_=hb_, pk_=pk_: e.tensor_copy(out=kTn[hb_][:, 0:NT], in_=pk_[:, 0:NT]),
                          reads=[pkk], writes=[("kTn", hb_)])
                    P.add("sp", lambda e, hb_=hb_, h=h: e.dma_start(
                        out=KTd[h, :, tok0:tok0 + NT], in_=kTn[hb_][:, 0:NT]),
                        reads=[("kTn", hb_)], writes=[("KTd", h)], dma_key=("kst", hb_))
                    pz, pzk = pa_next()
                    proj_feat(W, wk, 256, hnT, HK, 8, NT, pz, pzk)
                    P.add("act", lambda e, hb_=hb_, pz=pz: e.activation(out=sfz[hb_][:, 0:NT], in_=pz[:, 0:NT],
                                                                       func=AF.Tanh, scale=0.5),
                          reads=[pzk], writes=[("sfz", hb_)])
                    P.add("dve", lambda e, hb_=hb_, pz=pz: e.scalar_tensor_tensor(
                        out=sfz[hb_][:, 0:NT], in0=sfz[hb_][:, 0:NT], scalar=1.0, in1=pz[:, 0:NT],
                        op0=ALU.add, op1=ALU.mult),
                        reads=[pzk, ("sfz", hb_)], writes=[("sfz", hb_)])
                    P.add("dve", lambda e, hb_=hb_, h=h: e.tensor_scalar(
                        out=biasH[hb_][:, 0:nkb], in0=Fcum[:, 0:nkb, h], scalar1=-1.0, scalar2=Fref[:, h:h + 1],
                        op0=ALU.mult, op1=ALU.add),
                        reads=["Fcum", "Fref"], writes=[("biasH", hb_)])

                seg_slot = {}

                def fox_load(h, si):
                    s0 = si * KSEG
                    nb_ = min(KSEG, nkb - s0)
                    ks = nxt("kseg", NKS)
                    seg_slot[(h, si)] = ks
                    P.add("sp", lambda e: e.dma_start(
                        out=KTs[ks][:, 0:nb_ * 128], in_=KTd[h, :, s0 * 128:(s0 + nb_) * 128]),
                        reads=[("KTd", h)], writes=[("KTs", ks)], dma_key=("KTs", ks))
                    P.add("sp", lambda e: e.dma_start(
                        out=Vs[ks][:, 0:nb_, :], in_=Vd[h, :, s0:s0 + nb_, :]),
                        reads=["Vd"], writes=[("Vs", ks)], dma_key=("Vs", ks))

                nseg = (nkb + KSEG - 1) // KSEG
                load_list = [(hh, si) for hh in range(8) for si in range(nseg)]
                load_state = {"n": 0}

                def ensure_loads(h, si, maxh=None):
                    upto = load_list.index((h, si)) + KDIST
                    while load_state["n"] < len(load_list) and load_state["n"] <= upto:
                        hh, s2 = load_list[load_state["n"]]
                        if hh > (h + 1 if maxh is None else maxh):
                            break
                        fox_load(hh, s2)
                        load_state["n"] += 1

                def fox_attn(h):
                    hb_ = h % 2
                    blocks = []
                    for si in range(nseg):
                        s0 = si * KSEG
                        nb_ = min(KSEG, nkb - s0)
                        for jj in range(nb_):
                            blocks.append((si, jj, s0 + jj))
                    st_ = {}

                    def qk_exp(n):
                        si, jj, j = blocks[n]
                        if jj == 0:
                            ensure_loads(h, si)
                        ks = seg_slot[(h, si)]
                        diag = j >= T0
                        q0 = (j - T0) * 128 if diag else 0
                        pS, pSk = pa_next()
                        p_ = nxt("PT", 4)
                        st_[n] = (p_, q0)

                        def fn(e):
                            r = e.matmul(pS[:, q0:NT], lhsT=KTs[ks][:, jj * 128:(jj + 1) * 128],
                                         rhs=qT[hb_][:, q0:NT], start=True, stop=(not diag))
                            if diag:
                                r = e.matmul(pS[:, q0:q0 + 128], lhsT=ident[:], rhs=maskneg[:],
                                             start=False, stop=True)
                            return r
                        P.add("pe", fn, reads=[("KTs", ks), ("qT", hb_)] + CONST, writes=[pSk])
                        P.add("act", lambda e: e.activation(
                            out=PT[p_][:, q0:NT], in_=pS[:, q0:NT], func=AF.Exp, bias=biasH[hb_][:, j:j + 1]),
                            reads=[pSk, ("biasH", hb_)], writes=[("PT", p_)])

                    def pv(n):
                        si, jj, j = blocks[n]
                        ks = seg_slot[(h, si)]
                        p_, q0 = st_[n]

                        def fn2(e):
                            e.matmul(pO[:, q0:NT], lhsT=Vs[ks][:, jj, :], rhs=PT[p_][:, q0:NT],
                                     start=(j == 0), stop=(j == nkb - 1))
                            return e.matmul(pL[:, q0:NT], lhsT=ones_bf[:], rhs=PT[p_][:, q0:NT],
                                            start=(j == 0), stop=(j == nkb - 1))
                        P.add("pe", fn2, reads=[("Vs", ks), ("PT", p_)] + CONST, writes=["pO", "pL"])

                    nblk = len(blocks)
                    AHEAD = 3
                    for n in range(min(AHEAD, nblk)):
                        qk_exp(n)
                    for n in range(nblk):
                        pv(n)
                        if n + AHEAD < nblk:
                            qk_exp(n + AHEAD)
                    r_ = nxt("rl", 2)
                    P.add("dve", lambda e: e.tensor_copy(out=rl[r_][:, 0:NT], in_=pL[:, 0:NT]),
                          reads=["pL"], writes=[("rl", r_)])
                    P.add("dve", lambda e: e.tensor_copy(out=Osb[r_][:, 0:NT], in_=pO[:, 0:NT]),
                          reads=["pO"], writes=[("Osb", r_)])
                    P.add("dve", lambda e: e.reciprocal(out=rl[r_][:, 0:NT], in_=rl[r_][:, 0:NT]),
                          reads=[("rl", r_)], writes=[("rl", r_)])
                    P.add("dve", lambda e: e.scalar_tensor_tensor(
                        out=rl[r_][:, 0:NT], in0=rl[r_][:, 0:NT], scalar=0.5, in1=sfz[hb_][:, 0:NT],
                        op0=ALU.mult, op1=ALU.mult),
                        reads=[("rl", r_), ("sfz", hb_)], writes=[("rl", r_)])
                    P.add("dve", lambda e: e.tensor_tensor(
                        out=xaT[:, h, 0:NT], in0=Osb[r_][:, 0:NT], in1=rl[r_][:, 0:NT], op=ALU.mult),
                        reads=[("Osb", r_), ("rl", r_)], writes=["xaT"])


                def ml_items(h):
                    s_ = h % 2
                    ua, ub, uc, ud = U["ml"][h]
                    qk_, smzg_, vm_, sgo_ = qkTb[s_], smzgb[s_], vmb[s_], sgob[s_]

                    cst_ = {}

                    def conv_p(c):
                        WA, wak = w_get(ua)
                        ch = (2 * h + c) if c < 2 else (8 + 2 * h + (c - 2))
                        pc, pck = pa_next()
                        proj_feat(WA, wak, c * 128, hnT, HK, 8, NT, pc, pck)
                        sg = nxt("stage", 3)
                        cst_[c] = (sg, ch)
                        P.add("act", lambda e: e.copy(out=stage[sg][:, 3:3 + NT], in_=pc[:, 0:NT]),
                              reads=[pck], writes=[("stage", sg)])
                        P.add("dve", lambda e: e.tensor_copy(out=stage[sg][:, 0:3], in_=hist[:, ch, :]),
                              reads=[("hist", ch)], writes=[("stageh", sg)])

                    def conv_c(c):
                        sg, ch = cst_[c]
                        pv_, pvk = pa_next()

                        def fn(e):
                            r = None
                            for tp in range(4):
                                r = e.matmul(pv_[:, 0:NT], lhsT=cdiag[:, ch * 4 + tp, :], rhs=stage[sg][:, tp:tp + NT],
                                             start=(tp == 0), stop=(tp == 3))
                            return r
                        P.add("pe", fn, reads=[("stage", sg), ("stageh", sg), "cdiag"], writes=[pvk])
                        P.add("dve", lambda e: e.tensor_copy(out=hist[:, ch, :], in_=stage[sg][:, NT:NT + 3]),
                              reads=[("stage", sg), ("stageh", sg)], writes=[("hist", ch)])
                        P.add("act", lambda e: e.activation(out=qk_[:, c, 0:NT], in_=pv_[:, 0:NT], func=AF.Silu,
                                                            bias=sm[:, 80 + ch:81 + ch]),
                              reads=[pvk, "small"], writes=[("qkT", s_, c)])

                    def mz_chunk(c):
                        WB, wbk = w_get(ub)
                        pc, pck = pa_next()
                        proj_feat(WB, wbk, c * 128, hnT, HK, 8, NT, pc, pck)
                        z_ = nxt("smz", 2)
                        P.add("act", lambda e: e.activation(out=smz[z_][:, 0:NT], in_=pc[:, 0:NT], func=AF.Silu),
                              reads=[pck], writes=[("smz", z_)])
                        P.add("dve", lambda e: e.tensor_scalar(
                            out=smzg_[:, c, 0:NT], in0=smz[z_][:, 0:NT], scalar1=sm[:, 96 + h * 4 + c:97 + h * 4 + c],
                            scalar2=None, op0=ALU.mult),
                            reads=[("smz", z_), "small"], writes=[("smzg", s_, c)])

                    def v_tile(i):
                        WC, wck = w_get(uc)
                        pc, pck = pa_next()
                        proj_tok(WC, wck, 512, i, pc, pck, hnT, HK)
                        P.add("act", lambda e: e.copy(out=vm_[:, i, :], in_=pc[:, :]), reads=[pck], writes=[("vm", s_, i)])

                    def o_tile(i):
                        WD, wdk = w_get(ud)
                        pc, pck = pa_next()
                        proj_tok(WD, wdk, 512, i, pc, pck, hnT, HK)
                        P.add("act", lambda e: e.activation(out=sgo_[:, i, :], in_=pc[:, :], func=AF.Sigmoid),
                              reads=[pck], writes=[("sgo", s_, i)])

                    items = []
                    for (kind, c) in (("p", 0), ("p", 1), ("c", 0), ("p", 2), ("c", 1), ("p", 3), ("c", 2), ("c", 3)):
                        items.append((lambda c=c: conv_p(c)) if kind == "p" else (lambda c=c: conv_c(c)))
                    for c in range(4):
                        items.append(lambda c=c: mz_chunk(c))
                    for i in range(nt):
                        items.append(lambda i=i: v_tile(i))
                    for i in range(nt):
                        items.append(lambda i=i: o_tile(i))
                    return items

                def ml_head(h, filler):
                    s_ = h % 2
                    qk_, smzg_, vm_, sgo_ = qkTb[s_], smzgb[s_], vmb[s_], sgob[s_]
                    QK = [("qkT", s_, c) for c in range(4)]

                    per_tile = (len(filler) * 3) // (4 * nt) if nt else 0

                    def fill(n):
                        while n > 0 and filler:
                            filler.pop(0)()
                            n -= 1

                    for i in range(nt):
                        cs = slice(i * 128, (i + 1) * 128)
                        kt = nxt("ktok", 2)
                        sT = nxt("STm", 2)

                        def fn(e, cs=cs):
                            e.transpose(out=ptr[:, 0, :], in_=qk_[:, 2, cs], identity=ident[:])
                            return e.transpose(out=ptr[:, 1, :], in_=qk_[:, 3, cs], identity=ident[:])
                        P.add("pe", fn, reads=[QK[2], QK[3]] + CONST, writes=["ptr"])
                        P.add("dve", lambda e, kt=kt, i=i: e.tensor_scalar(
                            out=ktok[kt][:, :], in0=ptr[:, 0:2, :].rearrange("p a b -> p (a b)"),
                            scalar1=gdsT[:, i, h:h + 1], scalar2=None, op0=ALU.mult),
                            reads=["ptr", ("gdsT", i)], writes=[("ktok", kt)])

                        def fn(e, cs=cs):
                            e.matmul(pO[:, 0:128], lhsT=qk_[:, 2, cs], rhs=qk_[:, 0, cs], start=True, stop=False)
                            return e.matmul(pO[:, 0:128], lhsT=qk_[:, 3, cs], rhs=qk_[:, 1, cs], start=False, stop=True)
                        P.add("pe", fn, reads=QK, writes=["pO"])
                        P.add("dve", lambda e, sT=sT, i=i: e.scalar_tensor_tensor(
                            out=STm[sT][:], in0=pO[:, 0:128], scalar=gsT[:, i, h:h + 1], in1=mask01[:],
                            op0=ALU.mult, op1=ALU.mult),
                            reads=["pO", ("gsT", i)] + CONST, writes=[("STm", sT)])
                        pn, pnk = pa_next()

                        def fn(e, sT=sT, i=i, cs=cs, pn=pn):
                            e.matmul(pn[:, :], lhsT=STm[sT][:], rhs=vm_[:, i, :], start=True, stop=False)
                            e.matmul(pn[:, :], lhsT=qk_[:, 0, cs], rhs=Cb[:, 2 * h, :], start=False, stop=False)
                            return e.matmul(pn[:, :], lhsT=qk_[:, 1, cs], rhs=Cb[:, 2 * h + 1, :], start=False, stop=True)
                        P.add("pe", fn, reads=[("STm", sT), ("vm", s_, i), QK[0], QK[1], ("Cb", 2 * h),
                                               ("Cb", 2 * h + 1)], writes=[pnk])

                        def fn(e, sT=sT, cs=cs):
                            e.matmul(psm[:, 128:129], lhsT=STm[sT][:], rhs=ones_bf[:, 0:1], start=True, stop=False)
                            e.matmul(psm[:, 128:129], lhsT=qk_[:, 0, cs], rhs=nb[:, 2 * h:2 * h + 1], start=False, stop=False)
                            return e.matmul(psm[:, 128:129], lhsT=qk_[:, 1, cs], rhs=nb[:, 2 * h + 1:2 * h + 2],
                                            start=False, stop=True)
                        P.add("pe", fn, reads=[("STm", sT), QK[0], QK[1], "nb"] + CONST, writes=["psm"])
                        P.add("act", lambda e, i=i: e.activation(
                            out=dtmp[:, 0:1], in_=psm[:, 128:129], func=AF.Abs, scale=ehT[:, i, h:h + 1]),
                            reads=["psm", ("ehT", i)], writes=["dtmp0"])
                        P.add("dve", lambda e: e.tensor_scalar(out=dtmp[:, 3:4], in0=dtmp[:, 0:1], scalar1=1.0,
                                                               scalar2=None, op0=ALU.max),
                              reads=["dtmp0"], writes=["dtmp3"])
                        P.add("dve", lambda e: e.reciprocal(out=dtmp[:, 1:2], in_=dtmp[:, 3:4]),
                              reads=["dtmp3"], writes=["dtmp1"])
                        P.add("dve", lambda e, i=i: e.tensor_tensor(out=dtmp[:, 2:3], in0=dtmp[:, 1:2],
                                                                   in1=ehT[:, i, h:h + 1], op=ALU.mult),
                              reads=["dtmp1", ("ehT", i)], writes=["dtmp2"])
                        P.add("dve", lambda e, i=i, pn=pn: e.scalar_tensor_tensor(
                            out=hb[:, i, :], in0=pn[:, :], scalar=dtmp[:, 2:3], in1=sgo_[:, i, :],
                            op0=ALU.mult, op1=ALU.mult),
                            reads=[pnk, "dtmp2", ("sgo", s_, i)], writes=[("hb", i)])
                        P.add("act", lambda e, i=i: e.activation(out=junk[:, 0:512], in_=hb[:, i, :], func=AF.Square,
                                                                 accum_out=ssq[:, i:i + 1]),
                              reads=[("hb", i)], writes=["junk", ("ssq", i)])
                        for c in range(2):
                            pd, pdk = pa_next()
                            P.add("pe", lambda e, kt=kt, c=c, i=i, pd=pd: e.matmul(
                                pd[:, :], lhsT=ktok[kt][:, c * 128:(c + 1) * 128], rhs=vm_[:, i, :], start=True, stop=True),
                                reads=[("ktok", kt), ("vm", s_, i)], writes=[pdk])
                            P.add("dve", lambda e, c=c, i=i, pd=pd: e.scalar_tensor_tensor(
                                out=Cf[:, 2 * h + c, :], in0=Cf[:, 2 * h + c, :], scalar=decT[:, i, h:h + 1],
                                in1=pd[:, :], op0=ALU.mult, op1=ALU.add),
                                reads=[pdk, ("Cf", 2 * h + c), ("decT", i)], writes=[("Cf", 2 * h + c)])
                            P.add("act", lambda e, c=c: e.copy(out=Cb[:, 2 * h + c, :], in_=Cf[:, 2 * h + c, :]),
                                  reads=[("Cf", 2 * h + c)], writes=[("Cb", 2 * h + c)])

                        def fn(e, kt=kt):
                            e.matmul(psm[:, 192:193], lhsT=ktok[kt][:, 0:128], rhs=ones_bf[:, 0:1], start=True, stop=True)
                            return e.matmul(psm[:, 193:194], lhsT=ktok[kt][:, 128:256], rhs=ones_bf[:, 0:1],
                                            start=True, stop=True)
                        P.add("pe", fn, reads=[("ktok", kt), "dtmp0"] + CONST, writes=["psm"])
                        P.add("dve", lambda e, i=i: e.scalar_tensor_tensor(
                            out=nf[:, 2 * h:2 * h + 2], in0=nf[:, 2 * h:2 * h + 2], scalar=decT[:, i, h:h + 1],
                            in1=psm[:, 192:194], op0=ALU.mult, op1=ALU.add),
                            reads=["psm", "nf", ("decT", i)], writes=["nf"])
                        P.add("dve", lambda e: e.tensor_copy(out=nb[:, 2 * h:2 * h + 2], in_=nf[:, 2 * h:2 * h + 2]),
                              reads=["nf"], writes=["nb"])
                        fill(per_tile)
                    P.add("pool", lambda e: e.tensor_scalar(out=ssq[:, 4:4 + nt], in0=ssq[:, 0:0 + nt],
                                                            scalar1=1.0 / 512, scalar2=EPS, op0=ALU.mult, op1=ALU.add),
                          reads=[("ssq", i) for i in range(nt)], writes=["ssq_s"])
                    P.add("pool", lambda e: e.tensor_tensor(out=ssq[:, 4:4 + nt], in0=ssq[:, 4:4 + nt],
                                                            in1=mhalf[:, 0:nt], op=ALU.pow),
                          reads=["ssq_s"] + CONST, writes=["ssq_s"])
                    hns = []
                    for i in range(nt):
                        hn_ = nxt("hbn", 2)
                        hns.append(hn_)
                        P.add("act", lambda e, hn_=hn_, i=i: e.activation(out=hbn[hn_][:], in_=hb[:, i, :], func=AF.Copy,
                                                                       scale=ssq[:, 4 + i:5 + i]),
                              reads=[("hb", i), "ssq_s"], writes=[("hbn", hn_)])
                    fill(10 ** 6)
                    for i in range(nt):
                        hn_ = hns[i]

                        def fn(e, hn_=hn_):
                            r = None
                            for c in range(4):
                                r = e.transpose(out=ptr[:, 4 + c, :], in_=hbn[hn_][:, c * 128:(c + 1) * 128],
                                                identity=ident[:])
                            return r
                        P.add("pe", fn, reads=[("hbn", hn_)] + CONST, writes=["ptr"])
                        P.add("dve", lambda e, i=i: e.tensor_tensor(
                            out=hbT[:, 4 * h:4 * h + 4, i * 128:(i + 1) * 128], in0=ptr[:, 4:8, :],
                            in1=smzg_[:, 0:4, i * 128:(i + 1) * 128], op=ALU.mult),
                            reads=["ptr"] + [("smzg", s_, c) for c in range(4)], writes=["hbT"])

                fox_proj(0)
                ensure_loads(0, 0, 0)
                for h in range(8):
                    if h + 1 < 8:
                        fox_proj(h + 1)
                    if h == 7:
                        for it in ml_items(0):
                            it()
                    fox_attn(h)
                pa_mode["n"] = 5
                for h in range(4):
                    filler = ml_items(h + 1) if h < 3 else gate_items(0)
                    ml_head(h, filler)
                pa_mode["n"] = 4


            def stageE():
                pa_mode["n"] = 6
                for hf in range(2):
                    uga, ugb, uwa, uwb0, uwb1 = U["half"][hf]
                    if hf == 1:
                        for it in gate_items(1):
                            it()
                    WAa, wak = w_get(uwa)
                    WB0, wb0k = w_get(uwb0, uwa)
                    WB1, wb1k = w_get(uwb1, uwa)
                    for mm in range(4):
                        m = hf * 4 + mm
                        pya, pyak = pa_next()
                        proj_feat(WAa, wak, mm * 128, xaT, "xaT", 8, NT, pya, pyak)
                        P.add("dve", lambda e, mm=mm, pya=pya, m=m: e.tensor_tensor(
                            out=etmpA[mm][:, 0:NT], in0=pya[:, 0:NT], in1=sga[:, m, 0:NT], op=ALU.mult),
                            reads=[pyak, ("sga", m)], writes=[("etmpA", mm)])
                    for mm in range(4):
                        m = hf * 4 + mm
                        pyb, pybk = pa_next()

                        def fn(e, mm=mm, pyb=pyb, WB0=WB0, WB1=WB1):
                            r = None
                            for k in range(16):
                                Wk = WB0 if k < 8 else WB1
                                r = e.matmul(pyb[:, 0:NT], lhsT=Wk[:, k % 8, mm * 128:(mm + 1) * 128], rhs=hbT[:, k, 0:NT],
                                             start=(k == 0), stop=(k == 15))
                            return r
                        P.add("pe", fn, reads=[wb0k, wb1k, "hbT"], writes=[pybk])
                        t1 = nxt("etmp", 2)
                        P.add("dve", lambda e, t1=t1, pyb=pyb, m=m: e.tensor_tensor(
                            out=etmp[t1][:, 0:NT], in0=pyb[:, 0:NT], in1=sgb[:, m, 0:NT], op=ALU.mult),
                            reads=[pybk, ("sgb", m)], writes=[("etmp", t1)])
                        P.add("dve", lambda e, mm=mm, t1=t1, m=m: e.tensor_tensor(
                            out=mrg[:, m, 0:NT], in0=etmpA[mm][:, 0:NT], in1=etmp[t1][:, 0:NT], op=ALU.add),
                            reads=[("etmpA", mm), ("etmp", t1)], writes=["mrg"])
                WO0, wo0k = w_get(U["wo"][0])
                WO1, wo1k = w_get(U["wo"][1], U["wo"][0])
                pouts = []
                for i, T in enumerate(tiles):
                    pp = []
                    for j, (WO, wok) in enumerate(((WO0, wo0k), (WO1, wo1k))):
                        po, pok = pa_next()

                        def fn(e, i=i, WO=WO, po=po):
                            r = None
                            for k in range(8):
                                r = e.matmul(po[:, :], lhsT=mrg[:, k, i * 128:(i + 1) * 128], rhs=WO[:, k, :],
                                             start=(k == 0), stop=(k == 7))
                            return r
                        P.add("pe", fn, reads=[wok, "mrg"], writes=[pok])
                        P.add("act", lambda e, po=po, i=i, j=j: e.activation(
                            out=junk[:, 0:512], in_=po[:, :], func=AF.Square, accum_out=ssE[:, 2 * i + j:2 * i + j + 1]),
                            reads=[pok], writes=["junk", ("ssE", 2 * i + j)])
                        pp.append((po, pok))
                    pouts.append(pp)
                    P.add("dve", lambda e, i=i: e.tensor_tensor(out=ssE[:, 4 + i:5 + i], in0=ssE[:, 2 * i:2 * i + 1],
                                                               in1=ssE[:, 2 * i + 1:2 * i + 2], op=ALU.add),
                          reads=[("ssE", 2 * i), ("ssE", 2 * i + 1)], writes=[("ssE_t", i)])
                P.add("pool", lambda e: e.tensor_scalar(out=ssE[:, 6:6 + nt], in0=ssE[:, 4:4 + nt],
                                                        scalar1=1.0 / D, scalar2=EPS, op0=ALU.mult, op1=ALU.add),
                      reads=[("ssE_t", i) for i in range(nt)], writes=["ssE_s"])
                P.add("pool", lambda e: e.tensor_tensor(out=ssE[:, 6:6 + nt], in0=ssE[:, 6:6 + nt],
                                                        in1=mhalf[:, 0:nt], op=ALU.pow),
                      reads=["ssE_s"] + CONST, writes=["ssE_s"])
                for i, T in enumerate(tiles):
                    xr = nxt("xres", 2)
                    ot = nxt("otile", 2)
                    P.add("sp", lambda e, xr=xr, T=T: e.dma_start(out=xres[xr][:], in_=H_in[T * 128:(T + 1) * 128, :]),
                          reads=[(hin_name, T)], writes=[("xres", xr)], dma_key=("xres", xr))
                    for j in range(2):
                        po, pok = pouts[i][j]
                        P.add("dve", lambda e, ot=ot, po=po, i=i, j=j: e.scalar_tensor_tensor(
                            out=otile[ot][:, j * 512:(j + 1) * 512], in0=po[:, :], scalar=ssE[:, 6 + i:7 + i],
                            in1=gpost[:, j * 512:(j + 1) * 512], op0=ALU.mult, op1=ALU.mult),
                            reads=[pok, "ssE_s", "gpost"], writes=[("otile", ot, j)])
                    P.add("dve", lambda e, ot=ot, xr=xr: e.tensor_tensor(out=otile[ot][:], in0=otile[ot][:],
                                                                         in1=xres[xr][:], op=ALU.add),
                          reads=[("otile", ot, 0), ("otile", ot, 1), ("xres", xr)],
                          writes=[("otile", ot, 0), ("otile", ot, 1)])
                    P.add("sp", lambda e, ot=ot, T=T: e.dma_start(out=H_out[T * 128:(T + 1) * 128, :], in_=otile[ot][:]),
                          reads=[("otile", ot, 0), ("otile", ot, 1)], writes=[(hout_name, T)], dma_key=("ost", ot))
                pa_mode["n"] = 4

            return stageA, body, stageE

        return pre_A, pre_body, [make_group(g, tiles, (par0 + g) % 2) for g, tiles in enumerate(groups)]

    layers = []
    par0 = 0
    for l in range(depth):
        H_in, hin = (x_in, "Hx") if l == 0 else (Hs, "Hs")
        H_out, hout = (out_d, "Hout") if l == depth - 1 else (Hs, "Hs")
        layers.append(layer(l, H_in, hin, H_out, hout, par0))
        par0 = (par0 + len(groups)) % 2
    steps = []
    for l in range(depth):
        pre_A, pre_body, gl = layers[l]
        cch = conv_chunks(l + 1, len(gl)) if l + 1 < depth else [None] * len(gl)
        for g, (sA, bd, sE) in enumerate(gl):
            def bd2(bd=bd, cc=cch[g]):
                if cc is not None:
                    cc()
                bd()
            steps.append((g, pre_A, pre_body, sA, bd2, sE))
    steps[0][1]()
    steps[0][3]()
    for idx, (g, pre_A, pre_body, sA, bd, sE) in enumerate(steps):
        if g == 0:
            pre_body()
        bd()
        if idx + 1 < len(steps):
            ng, npre_A, _, nsA, _, _ = steps[idx + 1]
            if ng == 0:
                npre_A()
            nsA()
        sE()

    P.emit(nc, st, final_keys=[("ost", 0), ("ost", 1)])
    st.close()
    return nc


def prep_inputs(x, meta_tokens, norm_pre, norm_post, w_in, b_fox_f, conv_w, conv_b, b_mlstm_i, b_mlstm_f,
                mlstm_head_norm, w_a, w_b, w_o, ntiles, depth):
    NTOK = ntiles * 128
    B = x.shape[0]
    f32 = np.float32
    gpre = np.ascontiguousarray(np.broadcast_to(np.asarray(norm_pre, f32)[:depth, None, :], (depth, 128, D)))
    gpost = np.ascontiguousarray(np.broadcast_to(np.asarray(norm_post, f32)[:depth, None, :], (depth, 128, D)))
    small = np.zeros((128, depth, 112), f32)
    for l in range(depth):
        gb = np.concatenate([np.asarray(b_fox_f[l], f32), np.asarray(b_mlstm_i[l], f32), np.asarray(b_mlstm_f[l], f32)])
        small[:, l, 0:16] = gb[None, :]
        cw = np.asarray(conv_w[l], f32).reshape(4, 16, 128).transpose(2, 1, 0)
        small[:, l, 16:80] = cw.reshape(128, 64)
        small[:, l, 80:96] = np.asarray(conv_b[l], f32).reshape(16, 128).T
        small[:, l, 96:112] = np.asarray(mlstm_head_norm[l], f32).reshape(16, 128).T
    shared = {
        "gpre": gpre, "gpost": gpost, "small": small,
        "w_in": np.ascontiguousarray(np.asarray(w_in, f32)[:depth]),
        "w_a": np.ascontiguousarray(np.asarray(w_a, f32)[:depth]),
        "w_b": np.ascontiguousarray(np.asarray(w_b, f32)[:depth]),
        "w_o": np.ascontiguousarray(np.asarray(w_o, f32)[:depth]),
    }
    maps = []
    for b in range(B):
        xp = np.zeros((NTOK, D), f32)
        seq = np.concatenate([np.asarray(meta_tokens, f32), np.asarray(x[b], f32)], axis=0)
        n = min(NTOK, seq.shape[0])
        xp[:n] = seq[:n]
        m = dict(shared)
        m["x"] = xp
        maps.append(m)
    return maps


def run(inputs, ntiles, depth, trace=False):
    nc = build_program(ntiles, depth)
    maps = prep_inputs(ntiles=ntiles, depth=depth, **inputs)
    res = run_bass_kernel_spmd(nc, maps, core_ids=list(range(len(maps))), trace=trace)
    outs = [r["out"] for r in res.results]
    return np.stack(outs, axis=0), res


def kernel(**inputs):
    ntiles = (NMETA + SEQ + 127) // 128
    out, _ = run(inputs, ntiles, DEPTH)
    return np.ascontiguousarray(out[:, NMETA:NMETA + SEQ, :]).astype(np.float32)
```
